# Optimizing a Trainium2 kernel written in Bass

```python
import jax, jax.numpy as jnp
from jax import lax
import numpy as np

D_MODEL = 2048
BATCH = 8
SEQ = 2048
DEPTH = 1

CONV_DIM = D_MODEL // 2
CONV_WIDTH = 31
RET_HEADS = 8
RET_QK_DIM = D_MODEL // 2
RET_V_DIM = D_MODEL
QK_HEAD = RET_QK_DIM // RET_HEADS
V_HEAD = RET_V_DIM // RET_HEADS
CHUNK = 128
D_FF = 4 * D_MODEL
ROPE_BASE = 10000.0
EPS = 1e-6
N_BRANCH = 2
IN_COLS = 2 * CONV_DIM + 2 * RET_QK_DIM + 2 * RET_V_DIM + N_BRANCH * D_MODEL
N_MOD = 6

kernel_name = "hybrid_conformer_retention_adaln_block"


def rmsnorm(x, g):
    xf = x.astype(jnp.float32)
    y = xf * lax.rsqrt(jnp.mean(xf * xf, axis=-1, keepdims=True) + EPS)
    return (y * g.astype(jnp.float32)).astype(x.dtype)


def layernorm(x, g, b):
    xf = x.astype(jnp.float32)
    mu = jnp.mean(xf, axis=-1, keepdims=True)
    var = jnp.mean(jnp.square(xf - mu), axis=-1, keepdims=True)
    y = (xf - mu) * lax.rsqrt(var + EPS)
    return (y * g.astype(jnp.float32) + b.astype(jnp.float32)).astype(x.dtype)


def modulate(h, shift, scale):
    return h * (1.0 + scale[:, None, :]) + shift[:, None, :]


def rotary(x, positions):
    dh = x.shape[-1]
    half = dh // 2
    inv_freq = ROPE_BASE ** (-jnp.arange(0, half, dtype=jnp.float32) / half)
    ang = positions.astype(jnp.float32)[..., None] * inv_freq
    cos = jnp.cos(ang)[:, :, None, :]
    sin = jnp.sin(ang)[:, :, None, :]
    x1 = x[..., :half].astype(jnp.float32)
    x2 = x[..., half:].astype(jnp.float32)
    out = jnp.concatenate([x1 * cos - x2 * sin, x1 * sin + x2 * cos], axis=-1)
    return out.astype(x.dtype)


def conformer_conv_branch(u, conv_w, conv_b, ln_g, ln_b, w_proj):
    a, gt = u[..., :CONV_DIM], u[..., CONV_DIM:]
    v = a * jax.nn.sigmoid(gt)
    y = lax.conv_general_dilated(
        v, conv_w[:, None, :].astype(v.dtype), window_strides=(1,),
        padding=[(CONV_WIDTH - 1, 0)],
        dimension_numbers=("NWC", "WIO", "NWC"),
        feature_group_count=CONV_DIM) + conv_b
    y = jax.nn.silu(layernorm(y, ln_g, ln_b))
    return y @ w_proj


def retention_branch(q, k, v, g, positions, gn_g, gn_b, w_proj):
    B, S, _ = q.shape
    N = S // CHUNK
    q = rotary(q.reshape(B, S, RET_HEADS, QK_HEAD), positions)
    k = rotary(k.reshape(B, S, RET_HEADS, QK_HEAD), positions) * (QK_HEAD ** -0.5)
    v = v.reshape(B, S, RET_HEADS, V_HEAD)
    qc = q.reshape(B, N, CHUNK, RET_HEADS, QK_HEAD)
    kc = k.reshape(B, N, CHUNK, RET_HEADS, QK_HEAD)
    vc = v.reshape(B, N, CHUNK, RET_HEADS, V_HEAD)

    log_gamma = jnp.log(1.0 - jnp.exp2(-5.0 - jnp.arange(RET_HEADS, dtype=jnp.float32)))
    idx = jnp.arange(CHUNK, dtype=jnp.float32)
    diff = idx[:, None] - idx[None, :]
    decay_mask = jnp.where(diff[None] >= 0,
                           jnp.exp(jnp.maximum(diff, 0.0)[None] * log_gamma[:, None, None]),
                           0.0)
    zeta = jnp.exp((CHUNK - 1 - idx)[None, :] * log_gamma[:, None])
    xi = jnp.exp((idx + 1.0)[None, :] * log_gamma[:, None])
    chunk_decay = jnp.exp(CHUNK * log_gamma)

    scores = jnp.einsum("bnqhd,bnkhd->bnhqk", qc, kc) * decay_mask[None, None].astype(qc.dtype)
    inner = jnp.einsum("bnhqk,bnkhv->bnqhv", scores, vc)

    kv_chunk = jnp.einsum("bnkhd,hk,bnkhv->bnhdv", kc, zeta.astype(kc.dtype), vc)
    kv_seq = jnp.moveaxis(kv_chunk, 1, 0)
    cd = chunk_decay.astype(kv_seq.dtype)[None, :, None, None]

    def step(R, kv):
        return cd * R + kv, R

    _, states = lax.scan(step, jnp.zeros_like(kv_seq[0]), kv_seq)
    states = jnp.moveaxis(states, 0, 1)
    cross = jnp.einsum("bnqhd,hq,bnhdv->bnqhv", qc, xi.astype(qc.dtype), states)

    o = (inner + cross).reshape(B, S, RET_HEADS, V_HEAD)
    of = o.astype(jnp.float32)
    mu = jnp.mean(of, axis=-1, keepdims=True)
    var = jnp.mean(jnp.square(of - mu), axis=-1, keepdims=True)
    on = ((of - mu) * lax.rsqrt(var + EPS)).reshape(B, S, RET_V_DIM)
    on = (on * gn_g.astype(jnp.float32) + gn_b.astype(jnp.float32)).astype(q.dtype)
    return (jax.nn.silu(g) * on) @ w_proj


def setup_inputs(seed: int = 0) -> dict:
    key = jax.random.key(seed)
    ks = jax.random.split(key, 20)
    f32 = jnp.float32
    D = D_MODEL

    def nrm(k, shape, fan_in, scale=1.0):
        return jax.random.normal(k, shape, f32) * (scale * fan_in ** -0.5)

    x = jax.random.normal(ks[0], (BATCH, SEQ, D), f32)
    c = jax.random.normal(ks[1], (BATCH, D), f32)
    positions = jnp.broadcast_to(jnp.arange(SEQ, dtype=jnp.int32), (BATCH, SEQ))
    return {
        "x": x,
        "c": c,
        "positions": positions,
        "w_ada": nrm(ks[2], (DEPTH, D, N_MOD * D), D, 0.5),
        "b_ada": 0.01 * jax.random.normal(ks[3], (DEPTH, N_MOD * D), f32),
        "g_norm_mix": 1.0 + 0.02 * jax.random.normal(ks[4], (DEPTH, D), f32),
        "w_in": nrm(ks[5], (DEPTH, D, IN_COLS), D),
        "conv_w": nrm(ks[6], (DEPTH, CONV_WIDTH, CONV_DIM), CONV_WIDTH),
        "conv_b": 0.01 * jax.random.normal(ks[7], (DEPTH, CONV_DIM), f32),
        "conv_ln_g": 1.0 + 0.02 * jax.random.normal(ks[8], (DEPTH, CONV_DIM), f32),
        "conv_ln_b": 0.01 * jax.random.normal(ks[9], (DEPTH, CONV_DIM), f32),
        "w_conv_out": nrm(ks[10], (DEPTH, CONV_DIM, D), CONV_DIM),
        "ret_gn_g": 1.0 + 0.02 * jax.random.normal(ks[11], (DEPTH, RET_V_DIM), f32),
        "ret_gn_b": 0.01 * jax.random.normal(ks[12], (DEPTH, RET_V_DIM), f32),
        "w_ret_out": nrm(ks[13], (DEPTH, RET_V_DIM, D), RET_V_DIM),
        "w_out": nrm(ks[14], (DEPTH, D, D), D),
        "g_norm_ffn": 1.0 + 0.02 * jax.random.normal(ks[15], (DEPTH, D), f32),
        "w_ff1": nrm(ks[16], (DEPTH, D, D_FF), D),
        "w_ff2": nrm(ks[17], (DEPTH, D_FF, D), D_FF),
        "g_norm_final": 1.0 + 0.02 * jax.random.normal(ks[18], (D,), f32),
    }


def reference(x, c, positions, w_ada, b_ada, g_norm_mix, w_in, conv_w, conv_b,
              conv_ln_g, conv_ln_b, w_conv_out, ret_gn_g, ret_gn_b, w_ret_out,
              w_out, g_norm_ffn, w_ff1, w_ff2, g_norm_final):
    D = D_MODEL
    c_act = jax.nn.silu(c)
    o_conv = 2 * CONV_DIM
    o_q = o_conv + RET_QK_DIM
    o_k = o_q + RET_QK_DIM
    o_v = o_k + RET_V_DIM
    o_g = o_v + RET_V_DIM
    o_ga = o_g + D
    for l in range(DEPTH):
        mod = c_act @ w_ada[l] + b_ada[l]
        shift_m, scale_m, gate_m = mod[:, :D], mod[:, D:2 * D], mod[:, 2 * D:3 * D]
        shift_f, scale_f, gate_f = mod[:, 3 * D:4 * D], mod[:, 4 * D:5 * D], mod[:, 5 * D:]

        h = modulate(rmsnorm(x, g_norm_mix[l]), shift_m, scale_m)
        proj = h @ w_in[l]
        y_conv = conformer_conv_branch(proj[..., :o_conv], conv_w[l], conv_b[l],
                                       conv_ln_g[l], conv_ln_b[l], w_conv_out[l])
        y_ret = retention_branch(proj[..., o_conv:o_q], proj[..., o_q:o_k],
                                 proj[..., o_k:o_v], proj[..., o_v:o_g], positions,
                                 ret_gn_g[l], ret_gn_b[l], w_ret_out[l])
        gate_conv = jax.nn.sigmoid(proj[..., o_g:o_ga])
        gate_ret = jax.nn.sigmoid(proj[..., o_ga:])
        merged = gate_conv * y_conv + gate_ret * y_ret
        x = x + gate_m[:, None, :] * (merged @ w_out[l])

        h = modulate(rmsnorm(x, g_norm_ffn[l]), shift_f, scale_f)
        ff = jnp.square(jax.nn.relu(h @ w_ff1[l])) @ w_ff2[l]
        x = x + gate_f[:, None, :] * ff
    return rmsnorm(x, g_norm_final)
```

```python
import numpy as np
import concourse.bass as bass
import concourse.mybir as mybir
from concourse.bass_utils import run_bass_kernel_spmd

F32 = mybir.dt.float32
BF16 = mybir.dt.bfloat16
I32 = mybir.dt.int32
AF = mybir.ActivationFunctionType
ALU = mybir.AluOpType
AX = mybir.AxisListType


class Buf:
    __slots__ = ("name", "w", "r", "psum")

    def __init__(self, name, psum=False):
        self.name = name
        self.psum = psum
        self.w = None
        self.r = []


class _Recorder:
    def __getattr__(self, name):
        def rec(*args, **kwargs):
            return (name, args, kwargs)
        return rec


E = _Recorder()


class Sched:
    ENGS = ("pe", "dve", "act", "pool", "sp")

    def __init__(self, nc, same_engine_sync=True):
        self.nc = nc
        self.h = {"pe": nc.tensor, "dve": nc.vector, "act": nc.scalar,
                  "pool": nc.gpsimd, "sp": nc.sync}
        self.ops = {e: [] for e in self.ENGS}
        self.sem = {e: nc.alloc_semaphore("sem_" + e) for e in self.ENGS}
        self.cnt = {e: 0 for e in self.ENGS}
        self.sig = {e: [] for e in self.ENGS}
        self.waited = {e: {} for e in self.ENGS}
        self.same = same_engine_sync
        self.dma_sems = {}
        self.dma_cnt = {}

    def dma_sem(self, name):
        if name not in self.dma_sems:
            self.dma_sems[name] = self.nc.alloc_semaphore("dsem_" + name)
            self.dma_cnt[name] = 0
        return name

    def _resolve(self, tok):
        if tok[0] == "d":
            return (self.dma_sems[tok[1]], "D" + tok[1], self.dma_cnt[tok[1]])
        _, e, seq = tok
        best = None
        for (s, c) in reversed(self.sig[e]):
            if s < seq:
                break
            best = (s, c)
        if best is None:
            rec = self.ops[e][-1]
            assert rec["inc"] is None and rec["fn"] is not None and len(self.ops[e]) - 1 >= seq, tok
            self.cnt[e] += 1
            rec["inc"] = (self.sem[e], 1)
            best = (len(self.ops[e]) - 1, self.cnt[e])
            self.sig[e].append(best)
        return (self.sem[e], "E" + e, best[1])

    def _deps(self, eng, reads, writes, pe_accum=False):
        toks = []
        for b in reads:
            if b.w is not None:
                toks.append(b.w)
            if b.psum:
                toks.extend(t for t in b.r if not (t[0] == "c" and t[1] == eng))
        for b in writes:
            if b.w is not None:
                toks.append(b.w)
            toks.extend(b.r)
        waits = {}
        for t in toks:
            if t[0] == "c" and t[1] == eng:
                if not self.same or eng == "pe":
                    continue
            sem, key, val = self._resolve(t)
            if self.waited[eng].get(key, 0) >= val:
                continue
            if key not in waits or waits[key][1] < val:
                waits[key] = (sem, val)
        for key, (sem, val) in waits.items():
            self.waited[eng][key] = val
        return list(waits.values())

    def _commit(self, tok, reads, writes):
        for b in reads:
            b.r = [t for t in b.r if not (t[0] == tok[0] and t[1] == tok[1])]
            b.r.append(tok)
        for b in writes:
            b.w = tok
            b.r = []

    def op(self, eng, fn, reads=(), writes=(), signal=True):
        waits = self._deps(eng, reads, writes)
        seq = len(self.ops[eng])
        rec = {"fn": fn, "waits": waits, "inc": None}
        if signal:
            self.cnt[eng] += 1
            self.sig[eng].append((seq, self.cnt[eng]))
            rec["inc"] = (self.sem[eng], 1)
        self.ops[eng].append(rec)
        tok = ("c", eng, seq)
        self._commit(tok, reads, writes)
        return tok

    def dma(self, queue, out, in_, reads=(), writes=(), sem="misc", **kw):
        self.dma_sem(sem)
        waits = self._deps(queue, reads, writes)
        self.dma_cnt[sem] += 16
        val = self.dma_cnt[sem]
        semh = self.dma_sems[sem]

        def fn(e, out=out, in_=in_, kw=kw):
            return e.dma_start(out=out, in_=in_, **kw)
        self.ops[queue].append({"fn": fn, "waits": waits, "inc": (semh, 16)})
        tok = ("d", sem, val)
        self._commit(tok, reads, writes)
        return tok

    def wait_all(self, eng, toks):
        waits = []
        for t in toks:
            sem, key, val = self._resolve(t)
            if self.waited[eng].get(key, 0) >= val:
                continue
            self.waited[eng][key] = val
            waits.append((sem, val))
        self.ops[eng].append({"fn": None, "waits": waits, "inc": None})

    def emit(self):
        nc = self.nc
        with nc.Block() as block:
            def mk(ename):
                def body(e):
                    for rec in self.ops[ename]:
                        for (sem, val) in rec["waits"]:
                            e.wait_ge(sem, val)
                        if rec["fn"] is None:
                            continue
                        fn = rec["fn"]
                        if isinstance(fn, tuple):
                            ins = getattr(e, fn[0])(*fn[1], **fn[2])
                        else:
                            ins = fn(e)
                        if rec["inc"] is not None:
                            ins.then_inc(rec["inc"][0], rec["inc"][1])
                return body
            block.tensor(mk("pe"))
            block.vector(mk("dve"))
            block.scalar(mk("act"))
            block.gpsimd(mk("pool"))
            block.sync(mk("sp"))


D = 2048
SEQ = 2048
NB = 8
CONV_DIM = 1024
CONV_W = 31
HEADS = 8
DQK = 128
DV = 256
D_FF = 8192
EPS = 1e-6
TT = 512
NCH = TT // 128
NT = SEQ // TT
KC = D // 128
HALO = CONV_W - 1
O_CONV = 2 * CONV_DIM
O_Q = O_CONV
O_K = O_Q + 1024
O_V = O_K + 1024
O_G = O_V + 2048
O_GC = O_G + 2048
O_GR = O_GC + 2048
IN_COLS = O_GR + 2048
NSLOT = 3

R_BADA, R_C, R_GMIX = 0, 96, 112
R_GFFN, R_GFIN, R_GNG, R_GNB, R_CONVB, R_LNG, R_LNB = 128, 144, 160, 176, 192, 200, 208
R_CONVW = 256
NVR = 512
C_ID, C_PERM, C_ONES, C_MASK, C_XI, C_ZETA, C_INVF = 0, 128, 256, 384, 1408, 2432, 2440
NCONST = 2441


def host_consts():
    f32 = np.float32
    c = np.zeros((128, NCONST), f32)
    c[:, C_ID:C_ID + 128] = np.eye(128, dtype=f32)
    perm = np.zeros((128, 128), f32)
    for m in range(128):
        if m < 64:
            perm[m + 64, m] = -1.0
        else:
            perm[m - 64, m] = 1.0
    c[:, C_PERM:C_PERM + 128] = perm
    c[:, C_ONES:C_ONES + 128] = 1.0
    hh = np.arange(HEADS, dtype=f32)
    log_gamma = np.log(f32(1.0) - np.exp2(f32(-5.0) - hh)).astype(f32)
    idx = np.arange(128, dtype=f32)
    scale = f32(DQK ** -0.5)
    for h in range(HEADS):
        diff = idx[None, :] - idx[:, None]
        m = np.where(diff >= 0, np.exp(np.maximum(diff, 0.0) * log_gamma[h]), 0.0).astype(f32)
        c[:, C_MASK + h * 128:C_MASK + (h + 1) * 128] = m * scale
        xi = np.exp((idx + 1.0) * log_gamma[h]).astype(f32)
        c[:, C_XI + h * 128:C_XI + (h + 1) * 128] = xi[None, :]
        zeta = np.exp((127.0 - idx) * log_gamma[h]).astype(f32)
        c[:, C_ZETA + h] = zeta * scale
    half = 64
    invf = (f32(10000.0) ** (-np.arange(0, half, dtype=f32) / f32(half))).astype(f32)
    c[:, C_INVF] = np.concatenate([invf, invf])
    cd = np.exp(f32(128.0) * log_gamma).astype(f32)
    return c, [float(v) for v in cd]


def build_program(dbg=None, stop=None):
    nc = bass.Bass("TRN2", target_bir_lowering=False)
    _, CD = host_consts()

    def din(name, shape, dt=F32):
        return nc.dram_tensor(name, shape, dt, kind="ExternalInput").ap()
    x = din("x", [SEQ, D])
    pos = din("pos", [1, SEQ], I32)
    vecs = din("vecs", [NVR, 128])
    consts = din("consts", [128, NCONST])
    w_ada = din("w_ada", [D, 6 * D])
    w_in = din("w_in", [D, IN_COLS])
    w_co = din("w_conv_out", [CONV_DIM, D])
    w_ro = din("w_ret_out", [D, D])
    w_o = din("w_out", [D, D])
    w_f1 = din("w_ff1", [D, D_FF])
    w_f2 = din("w_ff2", [D_FF, D])
    out = nc.dram_tensor("out", [SEQ, D], F32, kind="ExternalOutput").ap()
    dbg_t = {}
    if dbg:
        for k, shp in dbg.items():
            dbg_t[k] = nc.dram_tensor("dbg_" + k, shp, F32, kind="ExternalOutput").ap()

    S = Sched(nc)

    def sb(name, shape, dt=F32):
        return nc.alloc_sbuf_tensor(name, shape, dt)

    cst = sb("cst", [128, NCONST]); cstB = Buf("cst")
    vfm = sb("vfm", [128, NVR]); vfmB = Buf("vfm")
    modv = sb("modv", [128, 96]); modB = Buf("modv")
    prm = sb("prm", [128, 4, 16]); prmB = Buf("prm")
    cact = sb("cact", [128, 16], BF16); cactB = Buf("cact")
    xT = sb("xT", [128, KC, TT]); xTB = [Buf(f"xT{i}") for i in range(KC)]
    xTcB = [Buf(f"xTc{c}") for c in range(NCH)]
    hT = sb("hT", [128, KC, TT], BF16); hTB = [Buf(f"hT{i}") for i in range(KC)]
    wsl = [sb(f"wsl{i}", [128, 16, 512], BF16) for i in range(NSLOT)]
    wslB = [Buf(f"wsl{i}") for i in range(NSLOT)]
    R32 = sb("R32", [128, HEADS, DV]); R32B = [Buf(f"R32_{h}") for h in range(HEADS)]
    Rbf = sb("Rbf", [128, HEADS, DV], BF16); RbfB = [Buf(f"Rbf_{h}") for h in range(HEADS)]
    halo = sb("halo", [128, 8, HALO]); haloB = [Buf(f"halo{i}") for i in range(8)]
    cosT = sb("cosT", [128, TT]); sinT = sb("sinT", [128, TT]); trigB = Buf("trig")
    rrep = sb("rrep", [128, TT]); rrepB = Buf("rrep")
    mrep = sb("mrep", [128, TT]); mrepB = Buf("mrep")
    NTMP = 4
    tmp = [sb(f"tmp{i}", [128, TT]) for i in range(NTMP)]
    tmpB = [Buf(f"tmp{i}") for i in range(NTMP)]
    sml = sb("sml", [128, 64]); smlB = Buf("sml")
    XBYTES = 65 * 1024
    X = sb("X", [128, XBYTES // 2], BF16)
    psum = [nc.alloc_psum_tensor(f"ps{i}", [128, 512], F32) for i in range(8)]
    psB = [Buf(f"ps{i}", psum=True) for i in range(8)]
    st = {"ps": 0, "tmp": 0, "w": 0, "alt": 0}

    def newps():
        i = st["ps"]; st["ps"] = (i + 1) % 8
        return psum[i], psB[i]

    def newtmp():
        i = st["tmp"]; st["tmp"] = (i + 1) % NTMP
        return tmp[i], tmpB[i]

    def xview(off, free, dt):
        n = int(np.prod(free)); esz = 4 if dt == F32 else 2
        assert off % 4 == 0 and off + n * esz <= XBYTES, (off, free)
        a = X[:, off // 2: off // 2 + n * esz // 2]
        if dt == F32:
            a = a.bitcast(F32)
        if len(free) == 2:
            a = a.rearrange("p (a b) -> p a b", a=free[0])
        return a

    xregion = {"bufs": []}

    def xbufs(names, inherit):
        bs = []
        for n in names:
            b = Buf(n)
            b.r = list(inherit)
            bs.append(b)
        xregion["bufs"].extend(bs)
        return bs

    def xcollect():
        toks = {}
        for b in xregion["bufs"]:
            for t in ([b.w] if b.w is not None else []) + list(b.r):
                key = (t[0], t[1])
                if key not in toks or toks[key][2] < t[2]:
                    toks[key] = t
        xregion["bufs"] = []
        return list(toks.values())

    def wload(parts):
        i = st["w"]; st["w"] = (i + 1) % NSLOT
        for (ap, coff) in parts:
            K, n = ap.shape
            nk = K // 128
            S.dma("pool", wsl[i][:, 0:nk, coff:coff + n],
                  ap.rearrange("(kc p) n -> p kc n", p=128),
                  writes=[wslB[i]], sem=f"w{i}")
        return wsl[i], wslB[i]

    def copy_alt(out_ap, in_ap, reads, writes):
        st["alt"] ^= 1
        if st["alt"]:
            return S.op("dve", E.tensor_copy(out_ap, in_ap), reads=reads, writes=writes)
        return S.op("act", E.copy(out_ap, in_ap), reads=reads, writes=writes)

    ident = cst[:, C_ID:C_ID + 128]
    ones = cst[:, C_ONES:C_ONES + 128]
    permT = cst[:, C_PERM:C_PERM + 128]

    S.dma("sp", cst[:], consts, writes=[cstB], sem="c0")
    vst, vstB = xview(0, (4, 128), F32), None
    (vstB,) = xbufs(["vst"], [])
    S.dma("sp", vst, vecs.rearrange("(b p) f -> p b f", p=128), writes=[vstB], sem="c1")
    p0, p0B = newps()
    for blk in range(4):
        S.op("pe", E.transpose(p0[:, blk * 128:(blk + 1) * 128], vst[:, blk, :], ident),
             reads=[vstB, cstB], writes=[p0B], signal=(blk == 3))
    S.op("dve", E.tensor_copy(vfm[:], p0[:]), reads=[p0B], writes=[vfmB])
    S.op("act", E.activation(cact[:], vfm[:, R_C:R_C + 16], AF.Silu), reads=[vfmB], writes=[cactB])
    for hh in range(HEADS):
        S.op("dve", E.memset(R32[:, hh, :], 0.0), writes=[R32B[hh]])
        S.op("dve", E.memset(Rbf[:, hh, :], 0.0), writes=[RbfB[hh]])
    for cc in range(8):
        S.op("dve", E.memset(halo[:, cc, :], 0.0), writes=[haloB[cc]])
    def ada_part(j0, j1):
        p, pB = newps()
        for j in range(j0, j1):
            wt, wtB = wload([(w_ada[:, j * 512:(j + 1) * 512], 0)])
            for oc in range(4):
                col = (j - j0) * 4 + oc
                for kc in range(KC):
                    S.op("pe", E.matmul(
                        p[:, col:col + 1], wt[:, kc, oc * 128:(oc + 1) * 128], cact[:, kc:kc + 1],
                        start=(kc == 0), stop=(kc == KC - 1)),
                        reads=[wtB, cactB], writes=[pB], signal=(kc == KC - 1))
        n = (j1 - j0) * 4
        S.op("dve", E.tensor_tensor(modv[:, j0 * 4:j1 * 4], p[:, 0:n], vfm[:, R_BADA + j0 * 4:R_BADA + j1 * 4], ALU.add),
             reads=[pB, vfmB], writes=[modB])

    A1 = prm[:, 0, :]; B1 = modv[:, 0:16]; GM = modv[:, 32:48]
    A2 = prm[:, 1, :]; B2 = modv[:, 48:64]; GF = modv[:, 80:96]
    GFIN = vfm[:, R_GFIN:R_GFIN + 16]

    def rstd_from_ps(ps_ap, psb, inv_n, dst, dstB):
        S.op("dve", E.tensor_scalar(dst, ps_ap, inv_n, EPS, ALU.mult, ALU.add), reads=[psb], writes=[dstB])
        S.op("act", E.activation(dst, dst, AF.Sqrt), reads=[dstB], writes=[dstB])
        S.op("dve", E.reciprocal(dst, dst), reads=[dstB], writes=[dstB])

    def sumsq_rstd(src, srcB):
        pss, pssB = newps()
        for fc in range(KC):
            t, tB = newtmp()
            S.op("act", E.activation(t[:], src[:, fc, :], AF.Square), reads=[srcB[fc]], writes=[tB])
            S.op("pe", E.matmul(pss[:], ones, t[:], start=(fc == 0), stop=(fc == KC - 1)),
                 reads=[tB, cstB], writes=[pssB], signal=(fc == KC - 1))
        rstd_from_ps(pss[:], pssB, 1.0 / D, rrep[:], rrepB)

    def norm_mod(A, B, rd):
        sumsq_rstd(xT, xTB)
        for fc in range(KC):
            t, tB = newtmp()
            S.op("dve", E.scalar_tensor_tensor(t[:], xT[:, fc, :], A[:, fc:fc + 1], rrep[:],
                                                                       ALU.mult, ALU.mult),
                 reads=[xTB[fc], rrepB] + rd, writes=[tB])
            S.op("act", E.activation(hT[:, fc, :], t[:], AF.Identity, bias=B[:, fc:fc + 1]),
                 reads=[tB] + rd, writes=[hTB[fc]])

    def dump(name, ap, bufs):
        if name in dbg_t:
            S.dma("sp", dbg_t[name], ap, reads=bufs, sem="dbg")

    posb = sb("posb", [128, TT], I32); posB = Buf("posb")
    acc2 = sb("acc2", [128, TT]); acc2B = Buf("acc2")

    def make_tables(tq0):
        posi = posb[:]
        S.dma("sp", posi, pos[0:1, tq0:tq0 + TT].partition_broadcast(128), writes=[posB], sem="pos")
        ang, angB = newtmp()
        S.op("dve", E.tensor_copy(ang[:], posi), reads=[posB], writes=[angB])
        S.op("dve", E.tensor_scalar(ang[:], ang[:], cst[:, C_INVF:C_INVF + 1], None, ALU.mult),
             reads=[angB, cstB], writes=[angB])
        TWO_PI = 2.0 * np.pi
        for (dst, shift) in ((sinT, 0.0), (cosT, np.pi / 2)):
            a, aB = newtmp(); ki, kiB = newtmp(); kf, kfB = newtmp()
            kint = ki[:].bitcast(I32)
            S.op("dve", E.tensor_scalar(a[:], ang[:], float(shift), None, ALU.add),
                 reads=[angB], writes=[aB])
            S.op("dve", E.tensor_scalar(kf[:], a[:], float(1.0 / TWO_PI), None, ALU.mult),
                 reads=[aB], writes=[kfB])
            S.op("dve", E.tensor_copy(kint, kf[:]), reads=[kfB], writes=[kiB])
            S.op("dve", E.tensor_copy(kf[:], kint), reads=[kiB], writes=[kfB])
            S.op("dve", E.scalar_tensor_tensor(a[:], kf[:], float(-TWO_PI), a[:], ALU.mult, ALU.add),
                 reads=[kfB, aB], writes=[aB])
            S.op("dve", E.tensor_scalar(kf[:], a[:], float(np.pi), float(-TWO_PI), ALU.is_gt, ALU.mult),
                 reads=[aB], writes=[kfB])
            S.op("dve", E.tensor_tensor(a[:], a[:], kf[:], ALU.add), reads=[aB, kfB], writes=[aB])
            S.op("dve", E.tensor_scalar(kf[:], a[:], float(-np.pi), float(TWO_PI), ALU.is_lt, ALU.mult),
                 reads=[aB], writes=[kfB])
            S.op("dve", E.tensor_tensor(a[:], a[:], kf[:], ALU.add), reads=[aB, kfB], writes=[aB])
            S.op("dve", E.tensor_scalar(a[:], a[:], float(np.pi), float(-np.pi), ALU.min, ALU.max),
                 reads=[aB], writes=[aB])
            S.op("act", E.activation(dst[:], a[:], AF.Sin), reads=[aB], writes=[trigB])

    out_toks = []
    make_tables(0)
    pre_xin = None

    def finish():
        tk = S.dma("sp", out[0:128, :], xT[:, 0:4, :].rearrange("p a b -> p (a b)"), reads=xTB, sem="out")
        S.wait_all("sp", [tk])
        S.emit()
        return nc

    if stop == "setup":
        return finish()
    for ti in range(NT):
        t0 = ti * TT
        inh = xcollect()
        xins = [xview(40 * 1024, (D,), F32), xview(48 * 1024, (D,), F32)]
        if pre_xin is None:
            xinBs = xbufs(["xin0", "xin1"], inh)
            nloaded = 0
        else:
            xinBs = pre_xin
            for b_ in xinBs:
                b_.r.extend(inh)
            xregion["bufs"].extend(xinBs)
            nloaded = 2
        for c in range(NCH):
            xin, xinB = xins[c % 2], xinBs[c % 2]
            if c >= nloaded:
                S.dma("sp", xin, x[t0 + c * 128:t0 + (c + 1) * 128, :], writes=[xinB], sem=f"xin{c % 2}")
            for g in range(4):
                ps, pB = newps()
                for j in range(4):
                    fc = g * 4 + j
                    S.op("pe", E.transpose(ps[:, j * 128:(j + 1) * 128],
                                                                         xin[:, fc * 128:(fc + 1) * 128], ident),
                         reads=[xinB, cstB], writes=[pB], signal=(j == 3))
                copy_alt(xT[:, g * 4:(g + 1) * 4, c * 128:(c + 1) * 128],
                         ps[:].rearrange("p (a b) -> p a b", a=4), [pB], xTB[g * 4:(g + 1) * 4] + [xTcB[c]])
        if ti == 0:
            dump("xT", xT[:], xTB)
            if stop == "in":
                return finish()
        if ti == 0:
            ada_part(0, 8)
            S.op("dve", E.scalar_tensor_tensor(prm[:, 0, :], modv[:, 16:32], 1.0, vfm[:, R_GMIX:R_GMIX + 16],
                                               ALU.add, ALU.mult), reads=[modB, vfmB], writes=[prmB])
        norm_mod(A1, B1, [prmB, modB])
        if ti == 0:
            dump("hT", hT[:], hTB)
            if stop == "norm1":
                return finish()
        inh = xcollect()
        yrin = xview(49 * 1024, (KC, TT), BF16)
        yrB = xbufs([f"yr{i}" for i in range(KC)], inh)
        qk32 = xview(0, (4, TT), F32)
        qrot = xview(8 * 1024, (2, TT), BF16)
        qxrot = xview(10 * 1024, (2, TT), BF16)
        krot = xview(12 * 1024, (2, TT), BF16)
        ktm = xview(14 * 1024, (NCH, 256), BF16)
        vtm = xview(16 * 1024, (NCH, 512), BF16)
        vz = xview(20 * 1024, (NCH, 512), BF16)
        o32 = xview(24 * 1024, (NCH, 512), F32)
        smk = xview(32 * 1024, (2, 256), BF16)
        qkB = xbufs([f"qk{j}" for j in range(4)], inh)
        qrB = xbufs(["qr0", "qr1"], inh); qxB = xbufs(["qx0", "qx1"], inh); krB = xbufs(["kr0", "kr1"], inh)
        ktB = xbufs([f"kt{c}" for c in range(NCH)], inh)
        vtB = xbufs([f"vt{c}" for c in range(NCH)], inh)
        vzB = xbufs([f"vz{c}" for c in range(NCH)], inh)
        oB = xbufs([f"o{c}" for c in range(NCH)], inh)
        smB = xbufs(["sm0", "sm1"], inh)
        sgv = xview(33 * 1024, (4, TT), F32)
        sgB = xbufs([f"sg{j}" for j in range(4)], inh)

        def qk_proj(hp):
            wt, wtB = wload([(w_in[:, O_Q + hp * 256:O_Q + (hp + 1) * 256], 0),
                             (w_in[:, O_K + hp * 256:O_K + (hp + 1) * 256], 256)])
            for j in range(4):
                p, pB = newps()
                for kc in range(KC):
                    S.op("pe", E.matmul(
                        p[:], wt[:, kc, j * 128:(j + 1) * 128], hT[:, kc, :], start=(kc == 0), stop=(kc == KC - 1)),
                        reads=[wtB, hTB[kc]], writes=[pB], signal=(kc == KC - 1))
                copy_alt(qk32[:, j, :], p[:], [pB], [qkB[j]])

        qk_proj(0)
        for hp in range(4):
            if ti == 0:
                ada_part(8 + hp * 4, 12 + hp * 4)
                if hp == 2:
                    S.op("dve", E.scalar_tensor_tensor(prm[:, 1, :], modv[:, 64:80], 1.0, vfm[:, R_GFFN:R_GFFN + 16],
                                                       ALU.add, ALU.mult), reads=[modB, vfmB], writes=[prmB])
            pperm = []
            for j in range(4):
                p, pB = newps()
                S.op("pe", E.matmul(p[:], permT, qk32[:, j, :], start=True, stop=True),
                     reads=[qkB[j], cstB], writes=[pB])
                pperm.append((p, pB))
            wtv, wtvB = wload([(w_in[:, O_V + hp * 512:O_V + (hp + 1) * 512], 0)])
            for j in range(4):
                p, pB = pperm[j]
                t1, t1B = newtmp(); t2, t2B = newtmp()
                S.op("dve", E.tensor_tensor(t1[:], qk32[:, j, :], cosT[:], ALU.mult),
                     reads=[qkB[j], trigB], writes=[t1B])
                S.op("dve", E.tensor_tensor(t2[:], p[:], sinT[:], ALU.mult),
                     reads=[pB, trigB], writes=[t2B])
                S.op("dve", E.tensor_tensor(qk32[:, j, :], t1[:], t2[:], ALU.add),
                     reads=[t1B, t2B], writes=[qkB[j]])
                if j < 2:
                    head = hp * 2 + j
                    S.op("act", E.copy(qrot[:, j, :], qk32[:, j, :]), reads=[qkB[j]], writes=[qrB[j]])
                    xi_b = cst[:, C_XI + head * 128:C_XI + (head + 1) * 128].unsqueeze(1).to_broadcast([128, NCH, 128])
                    S.op("dve", E.tensor_tensor(
                        qxrot[:, j, :].rearrange("p (c q) -> p c q", c=NCH),
                        qk32[:, j, :].rearrange("p (c q) -> p c q", c=NCH), xi_b, ALU.mult),
                        reads=[qkB[j], cstB], writes=[qxB[j]])
                else:
                    S.op("act", E.copy(krot[:, j - 2, :], qk32[:, j, :]), reads=[qkB[j]], writes=[krB[j - 2]])
            for c in range(NCH):
                p, pB = newps()
                for kc in range(KC):
                    S.op("pe", E.matmul(
                        p[:], hT[:, kc, c * 128:(c + 1) * 128], wtv[:, kc, :], start=(kc == 0), stop=(kc == KC - 1)),
                        reads=[wtvB, hTB[kc]], writes=[pB], signal=(kc == KC - 1))
                if c % 2 == 0:
                    S.op("act", E.copy(vtm[:, c, :], p[:]), reads=[pB], writes=[vtB[c]])
                else:
                    S.op("dve", E.tensor_copy(vtm[:, c, :], p[:]), reads=[pB], writes=[vtB[c]])
                for hh in range(2):
                    head = hp * 2 + hh
                    zc = cst[:, C_ZETA + head:C_ZETA + head + 1]
                    if c % 2 == 0:
                        S.op("act", E.mul(vz[:, c, hh * 256:(hh + 1) * 256], p[:, hh * 256:(hh + 1) * 256], zc),
                             reads=[pB, cstB], writes=[vzB[c]])
                    else:
                        S.op("dve", E.tensor_scalar(
                            vz[:, c, hh * 256:(hh + 1) * 256], p[:, hh * 256:(hh + 1) * 256], zc, None, ALU.mult),
                            reads=[pB, cstB], writes=[vzB[c]])
            for c in range(NCH):
                p, pB = newps()
                for hh in range(2):
                    S.op("pe", E.transpose(p[:, hh * 128:(hh + 1) * 128],
                                           qk32[:, 2 + hh, c * 128:(c + 1) * 128], ident),
                         reads=[qkB[2 + hh], cstB], writes=[pB], signal=(hh == 1))
                copy_alt(ktm[:, c, :], p[:, 0:256], [pB], [ktB[c]])
            wtg, wtgB = wload([(w_in[:, O_G + hp * 512:O_G + (hp + 1) * 512], 0)])
            S.op("dve", E.memset(sml[:, 0:48], 0.0), writes=[smlB], reads=[])
            for c in range(NCH):
                psc, pscB = newps()
                for hh in range(2):
                    S.op("pe", E.matmul(
                        psc[:, hh * 128:(hh + 1) * 128], krot[:, hh, c * 128:(c + 1) * 128],
                        qrot[:, hh, c * 128:(c + 1) * 128], start=True, stop=True),
                        reads=[krB[hh], qrB[hh]], writes=[pscB], signal=(hh == 1))
                si = c % 2
                S.op("dve", E.tensor_tensor(
                    smk[:, si, :], psc[:, 0:256], cst[:, C_MASK + hp * 256:C_MASK + (hp + 1) * 256], ALU.mult),
                    reads=[pscB, cstB], writes=[smB[si]])
                pk, pkB = newps()
                for hh in range(2):
                    S.op("pe", E.matmul(
                        pk[:, hh * 256:(hh + 1) * 256], ktm[:, c, hh * 128:(hh + 1) * 128],
                        vz[:, c, hh * 256:(hh + 1) * 256], start=True, stop=True),
                        reads=[ktB[c], vzB[c]], writes=[pkB], signal=(hh == 1))
                po, poB = newps()
                for hh in range(2):
                    head = hp * 2 + hh
                    S.op("pe", E.matmul(
                        po[:, hh * 256:(hh + 1) * 256], smk[:, si, hh * 128:(hh + 1) * 128],
                        vtm[:, c, hh * 256:(hh + 1) * 256], start=True, stop=False),
                        reads=[smB[si], vtB[c]], writes=[poB], signal=False)
                    S.op("pe", E.matmul(
                        po[:, hh * 256:(hh + 1) * 256], qxrot[:, hh, c * 128:(c + 1) * 128],
                        Rbf[:, head, :], start=False, stop=True),
                        reads=[qxB[hh], RbfB[head]], writes=[poB], signal=True)
                for hh in range(2):
                    head = hp * 2 + hh
                    S.op("dve", E.scalar_tensor_tensor(
                        R32[:, head, :], R32[:, head, :], CD[head], pk[:, hh * 256:(hh + 1) * 256], ALU.mult, ALU.add),
                        reads=[pkB], writes=[R32B[head]])
                    S.op("act", E.copy(Rbf[:, head, :], R32[:, head, :]),
                         reads=[R32B[head]], writes=[RbfB[head]])
                S.op("act", E.copy(o32[:, c, :], po[:]), reads=[poB], writes=[oB[c]])
                vc = c
                p, pB = newps()
                for kc in range(KC):
                    S.op("pe", E.matmul(
                        p[:], wtg[:, kc, vc * 128:(vc + 1) * 128], hT[:, kc, :], start=(kc == 0), stop=(kc == KC - 1)),
                        reads=[wtgB, hTB[kc]], writes=[pB], signal=(kc == KC - 1))
                S.op("act", E.activation(sgv[:, vc, :], p[:], AF.Silu), reads=[pB], writes=[sgB[vc]])
                for hh in range(2):
                    mi = 16 + (c * 2 + hh) * 2
                    S.op("dve", E.bn_stats(sml[:, hh * 6:(hh + 1) * 6], o32[:, c, hh * 256:(hh + 1) * 256]),
                         reads=[oB[c]], writes=[smlB])
                    S.op("dve", E.bn_aggr(sml[:, mi:mi + 2], sml[:, hh * 6:(hh + 1) * 6]),
                         reads=[smlB], writes=[smlB])
            if hp < 3:
                qk_proj(hp + 1)
            S.op("dve", E.tensor_scalar(sml[:, 32:40], sml[:, 17:33:2], EPS, None, ALU.add), reads=[smlB], writes=[smlB])
            S.op("act", E.activation(sml[:, 32:40], sml[:, 32:40], AF.Sqrt), reads=[smlB], writes=[smlB])
            S.op("dve", E.reciprocal(sml[:, 32:40], sml[:, 32:40]), reads=[smlB], writes=[smlB])
            for c in range(NCH):
                for hh in range(2):
                    mi = 16 + (c * 2 + hh) * 2
                    ri = 32 + c * 2 + hh
                    S.op("dve", E.tensor_scalar(
                        o32[:, c, hh * 256:(hh + 1) * 256], o32[:, c, hh * 256:(hh + 1) * 256],
                        sml[:, mi:mi + 1], sml[:, ri:ri + 1], ALU.subtract, ALU.mult),
                        reads=[smlB], writes=[oB[c]])
            for vc in range(4):
                fcv = hp * 4 + vc
                pt, ptB = newps()
                for c in range(NCH):
                    S.op("pe", E.transpose(pt[:, c * 128:(c + 1) * 128],
                                           o32[:, c, vc * 128:(vc + 1) * 128], ident),
                         reads=[oB[c], cstB], writes=[ptB], signal=(c == NCH - 1))
                t, tB = newtmp()
                S.op("dve", E.tensor_scalar(
                    t[:], pt[:], vfm[:, R_GNG + fcv:R_GNG + fcv + 1], vfm[:, R_GNB + fcv:R_GNB + fcv + 1], ALU.mult, ALU.add),
                    reads=[ptB, vfmB], writes=[tB])
                S.op("dve", E.tensor_tensor(yrin[:, fcv, :], t[:], sgv[:, vc, :], ALU.mult),
                     reads=[tB, sgB[vc]], writes=[yrB[fcv]])
        if ti == 0:
            if stop == "m2":
                return finish()
        keep = yrB
        xregion["bufs"] = [b for b in xregion["bufs"] if b not in keep]
        inh = xcollect()
        xregion["bufs"].extend(keep)
        merged = xview(0, (KC, TT), BF16)
        VB_OFF = 16 * 1024
        vbuf = xview(VB_OFF, (8, HALO + TT), F32)
        acc = xview(VB_OFF + 8 * (HALO + TT) * 4, (8, TT), F32)
        mgB = xbufs([f"mg{i}" for i in range(KC)], inh)
        vbB = xbufs([f"vb{i}" for i in range(8)], inh)
        accB = xbufs([f"acc{i}" for i in range(8)], inh)

        def conv_taps(cc):
            wcol = lambda k, cc=cc: vfm[:, R_CONVW + k * 8 + cc:R_CONVW + k * 8 + cc + 1]
            a2, a2B = acc2, acc2B
            S.op("dve", E.tensor_scalar(acc[:, cc, :], vbuf[:, cc, 0:TT], wcol(0),
                                        vfm[:, R_CONVB + cc:R_CONVB + cc + 1], ALU.mult, ALU.add),
                 reads=[vbB[cc], vfmB], writes=[accB[cc]])
            S.op("dve", E.tensor_scalar(a2[:], vbuf[:, cc, 1:1 + TT], wcol(1), None, ALU.mult),
                 reads=[vbB[cc], vfmB], writes=[a2B])
            for k in range(2, CONV_W):
                if k % 2 == 0:
                    S.op("dve", E.scalar_tensor_tensor(
                        acc[:, cc, :], vbuf[:, cc, k:k + TT], wcol(k), acc[:, cc, :], ALU.mult, ALU.add),
                        reads=[vbB[cc], vfmB], writes=[accB[cc]])
                else:
                    S.op("dve", E.scalar_tensor_tensor(
                        a2[:], vbuf[:, cc, k:k + TT], wcol(k), a2[:], ALU.mult, ALU.add),
                        reads=[vbB[cc], vfmB], writes=[a2B])
            S.op("dve", E.tensor_tensor(acc[:, cc, :], acc[:, cc, :], a2[:], ALU.add),
                 reads=[a2B], writes=[accB[cc]])

        for cp in range(4):
            wt, wtB = wload([(w_in[:, cp * 256:(cp + 1) * 256], 0),
                             (w_in[:, CONV_DIM + cp * 256:CONV_DIM + (cp + 1) * 256], 256)])
            pp = [newps() for _ in range(4)]
            for j in range(4):
                for kc in range(KC):
                    S.op("pe", E.matmul(
                        pp[j][0][:], wt[:, kc, j * 128:(j + 1) * 128], hT[:, kc, :], start=(kc == 0), stop=(kc == KC - 1)),
                        reads=[wtB, hTB[kc]], writes=[pp[j][1]], signal=(kc == KC - 1))
            for j in range(2):
                cc = cp * 2 + j
                sg, sgB = newtmp()
                S.op("act", E.activation(sg[:], pp[2 + j][0][:], AF.Sigmoid), reads=[pp[2 + j][1]], writes=[sgB])
                S.op("dve", E.tensor_copy(vbuf[:, cc, 0:HALO], halo[:, cc, :]), reads=[haloB[cc]], writes=[vbB[cc]])
                S.op("dve", E.tensor_tensor(vbuf[:, cc, HALO:HALO + TT], pp[j][0][:], sg[:], ALU.mult),
                     reads=[pp[j][1], sgB], writes=[vbB[cc]])
                S.op("dve", E.tensor_copy(halo[:, cc, :], vbuf[:, cc, TT:TT + HALO]), reads=[vbB[cc]], writes=[haloB[cc]])
            cg = cp
            wtg, wtgB = wload([(w_in[:, O_GR + cg * 512:O_GR + (cg + 1) * 512], 0)])
            wtr, wtrB = wload([(w_ro[:, cg * 512:(cg + 1) * 512], 0)])
            for oc in range(4):
                fc = cg * 4 + oc
                pg, pgB = newps()
                for kc in range(KC):
                    S.op("pe", E.matmul(pg[:], wtg[:, kc, oc * 128:(oc + 1) * 128], hT[:, kc, :],
                                        start=(kc == 0), stop=(kc == KC - 1)),
                         reads=[wtgB, hTB[kc]], writes=[pgB], signal=(kc == KC - 1))
                gt, gtB = newtmp()
                S.op("act", E.activation(gt[:], pg[:], AF.Sigmoid), reads=[pgB], writes=[gtB])
                py, pyB = newps()
                for kc in range(KC):
                    S.op("pe", E.matmul(py[:], wtr[:, kc, oc * 128:(oc + 1) * 128], yrin[:, kc, :],
                                        start=(kc == 0), stop=(kc == KC - 1)),
                         reads=[wtrB, yrB[kc]], writes=[pyB], signal=(kc == KC - 1))
                S.op("dve", E.tensor_tensor(merged[:, fc, :], py[:], gt[:], ALU.mult),
                     reads=[pyB, gtB], writes=[mgB[fc]])
                if oc == 1:
                    conv_taps(cp * 2)
            conv_taps(cp * 2 + 1)
        ycin = xview(VB_OFF, (8, TT), BF16)
        vtoks = {}
        for b_ in vbB:
            for t_ in ([b_.w] if b_.w is not None else []) + list(b_.r):
                key = (t_[0], t_[1])
                if key not in vtoks or vtoks[key][2] < t_[2]:
                    vtoks[key] = t_
        ycB = xbufs([f"yc{i}" for i in range(8)], list(vtoks.values()))
        gc0 = xview(VB_OFF + 8 * 1024, (4, TT), F32)
        gc0B = xbufs([f"gc0{i}" for i in range(4)], list(vtoks.values()))
        wtg0, wtg0B = wload([(w_in[:, O_GC:O_GC + 512], 0)])
        for oc in range(4):
            pg, pgB = newps()
            for kc in range(KC):
                S.op("pe", E.matmul(pg[:], wtg0[:, kc, oc * 128:(oc + 1) * 128], hT[:, kc, :],
                                    start=(kc == 0), stop=(kc == KC - 1)),
                     reads=[wtg0B, hTB[kc]], writes=[pgB], signal=(kc == KC - 1))
            S.op("act", E.activation(gc0[:, oc, :], pg[:], AF.Sigmoid), reads=[pgB], writes=[gc0B[oc]])
        ps1, ps1B = newps(); ps2, ps2B = newps()
        for cc in range(8):
            S.op("pe", E.matmul(ps1[:], ones, acc[:, cc, :], start=(cc == 0), stop=(cc == 7)),
                 reads=[accB[cc], cstB], writes=[ps1B], signal=(cc == 7))
        for cc in range(8):
            t, tB = newtmp()
            S.op("act", E.activation(t[:], acc[:, cc, :], AF.Square), reads=[accB[cc]], writes=[tB])
            S.op("pe", E.matmul(ps2[:], ones, t[:], start=(cc == 0), stop=(cc == 7)),
                 reads=[tB, cstB], writes=[ps2B], signal=(cc == 7))
        S.op("dve", E.tensor_scalar(mrep[:], ps1[:], 1.0 / CONV_DIM, None, ALU.mult), reads=[ps1B], writes=[mrepB])
        msq, msqB = newtmp()
        S.op("dve", E.tensor_tensor(msq[:], mrep[:], mrep[:], ALU.mult), reads=[mrepB], writes=[msqB])
        S.op("dve", E.scalar_tensor_tensor(rrep[:], ps2[:], 1.0 / CONV_DIM, msq[:], ALU.mult, ALU.subtract),
             reads=[ps2B, msqB], writes=[rrepB])
        S.op("dve", E.tensor_scalar(rrep[:], rrep[:], 0.0, EPS, ALU.max, ALU.add), reads=[rrepB], writes=[rrepB])
        S.op("act", E.activation(rrep[:], rrep[:], AF.Sqrt), reads=[rrepB], writes=[rrepB])
        S.op("dve", E.reciprocal(rrep[:], rrep[:]), reads=[rrepB], writes=[rrepB])
        for cc in range(8):
            t, tB = newtmp()
            S.op("dve", E.tensor_tensor(t[:], acc[:, cc, :], mrep[:], ALU.subtract),
                 reads=[accB[cc], mrepB], writes=[tB])
            S.op("dve", E.tensor_tensor(t[:], t[:], rrep[:], ALU.mult), reads=[tB, rrepB], writes=[tB])
            S.op("act", E.activation(ycin[:, cc, :], t[:], AF.Silu,
                                     bias=vfm[:, R_LNB + cc:R_LNB + cc + 1],
                                     scale=vfm[:, R_LNG + cc:R_LNG + cc + 1]),
                 reads=[tB, vfmB], writes=[ycB[cc]])
        for cg in range(4):
            if cg > 0:
                wtg, wtgB = wload([(w_in[:, O_GC + cg * 512:O_GC + (cg + 1) * 512], 0)])
            wtc, wtcB = wload([(w_co[:, cg * 512:(cg + 1) * 512], 0)])
            for oc in range(4):
                fc = cg * 4 + oc
                if cg > 0:
                    pg, pgB = newps()
                    for kc in range(KC):
                        S.op("pe", E.matmul(pg[:], wtg[:, kc, oc * 128:(oc + 1) * 128], hT[:, kc, :],
                                            start=(kc == 0), stop=(kc == KC - 1)),
                             reads=[wtgB, hTB[kc]], writes=[pgB], signal=(kc == KC - 1))
                    gt, gtB = newtmp()
                    S.op("act", E.activation(gt[:], pg[:], AF.Sigmoid), reads=[pgB], writes=[gtB])
                    gta = gt[:]
                else:
                    gta, gtB = gc0[:, oc, :], gc0B[oc]
                py, pyB = newps()
                for kc in range(8):
                    S.op("pe", E.matmul(py[:], wtc[:, kc, oc * 128:(oc + 1) * 128], ycin[:, kc, :],
                                        start=(kc == 0), stop=(kc == 7)),
                         reads=[wtcB, ycB[kc]], writes=[pyB], signal=(kc == 7))
                t, tB = newtmp()
                S.op("dve", E.tensor_tensor(t[:], py[:], gta, ALU.mult), reads=[pyB, gtB], writes=[tB])
                S.op("dve", E.tensor_tensor(merged[:, fc, :], t[:], merged[:, fc, :], ALU.add),
                     reads=[tB], writes=[mgB[fc]])
        for cg in range(4):
            wt, wtB = wload([(w_o[:, cg * 512:(cg + 1) * 512], 0)])
            for oc in range(4):
                fc = cg * 4 + oc
                p, pB = newps()
                for kc in range(KC):
                    S.op("pe", E.matmul(
                        p[:], wt[:, kc, oc * 128:(oc + 1) * 128], merged[:, kc, :], start=(kc == 0), stop=(kc == KC - 1)),
                        reads=[wtB, mgB[kc]], writes=[pB], signal=(kc == KC - 1))
                S.op("dve", E.scalar_tensor_tensor(
                    xT[:, fc, :], p[:], GM[:, fc:fc + 1], xT[:, fc, :], ALU.mult, ALU.add),
                    reads=[pB, modB], writes=[xTB[fc]])
        if ti == 0:
            dump("x1", xT[:], xTB)
            if stop == "m3":
                return finish()
        norm_mod(A2, B2, [prmB, modB])
        if ti + 1 < NT:
            make_tables((ti + 1) * TT)
        inh = xcollect()
        hid = xview(0, (64, TT), BF16)
        hidB = xbufs([f"hid{i}" for i in range(64)], inh)
        for j in range(16):
            wt, wtB = wload([(w_f1[:, j * 512:(j + 1) * 512], 0)])
            for oc in range(4):
                hc = j * 4 + oc
                p, pB = newps()
                for kc in range(KC):
                    S.op("pe", E.matmul(
                        p[:], wt[:, kc, oc * 128:(oc + 1) * 128], hT[:, kc, :], start=(kc == 0), stop=(kc == KC - 1)),
                        reads=[wtB, hTB[kc]], writes=[pB], signal=(kc == KC - 1))
                t, tB = newtmp()
                S.op("act", E.activation(t[:], p[:], AF.Relu), reads=[pB], writes=[tB])
                S.op("dve", E.tensor_tensor(hid[:, hc, :], t[:], t[:], ALU.mult),
                     reads=[tB], writes=[hidB[hc]])
        for cg in range(4):
            pp = [newps() for _ in range(4)]
            for kq in range(4):
                wt, wtB = wload([(w_f2[kq * 2048:(kq + 1) * 2048, cg * 512:(cg + 1) * 512], 0)])
                for oc in range(4):
                    for kc in range(KC):
                        first = (kq == 0 and kc == 0); last = (kq == 3 and kc == KC - 1)
                        S.op("pe", E.matmul(
                            pp[oc][0][:], wt[:, kc, oc * 128:(oc + 1) * 128], hid[:, kq * 16 + kc, :], start=first, stop=last),
                            reads=[wtB, hidB[kq * 16 + kc]], writes=[pp[oc][1]], signal=(kc == KC - 1))
            for oc in range(4):
                fc = cg * 4 + oc
                S.op("dve", E.scalar_tensor_tensor(
                    xT[:, fc, :], pp[oc][0][:], GF[:, fc:fc + 1], xT[:, fc, :], ALU.mult, ALU.add),
                    reads=[pp[oc][1], modB], writes=[xTB[fc]])
        if ti == 0 and stop == "ffn":
            return finish()
        inh_end = xcollect()
        pre_xin = None
        if ti + 1 < NT:
            pre_xin = xbufs(["xin0", "xin1"], inh_end)
            for c in range(2):
                S.dma("sp", xview((40 + 8 * c) * 1024, (D,), F32),
                      x[t0 + TT + c * 128:t0 + TT + (c + 1) * 128, :], writes=[pre_xin[c]], sem=f"xin{c}")
        sumsq_rstd(xT, xTB)
        for fc in range(KC):
            S.op("dve", E.scalar_tensor_tensor(xT[:, fc, :], xT[:, fc, :], GFIN[:, fc:fc + 1], rrep[:],
                                                                ALU.mult, ALU.mult),
                 reads=[rrepB, vfmB], writes=[xTB[fc]] + xTcB)
        ost = xview(0, (4, D), F32)
        ostB = xbufs([f"ost{i}" for i in range(4)], inh_end)
        for c in range(NCH):
            oi = c
            for g in range(4):
                ps, pB = newps()
                for j in range(4):
                    fc = g * 4 + j
                    S.op("pe", E.transpose(ps[:, j * 128:(j + 1) * 128],
                                                                              xT[:, fc, c * 128:(c + 1) * 128], ident),
                         reads=[xTcB[c], cstB], writes=[pB], signal=(j == 3))
                copy_alt(ost[:, oi, g * 512:(g + 1) * 512], ps[:], [pB], [ostB[oi]])
            out_toks.append(S.dma("sp", out[t0 + c * 128:t0 + (c + 1) * 128, :], ost[:, oi, :], reads=[ostB[oi]], sem="out"))
    S.wait_all("sp", out_toks[-1:])
    S.emit()
    return nc


def make_in_maps(inputs):
    f32 = np.float32
    consts, _ = host_consts()
    x = np.asarray(inputs["x"], f32)
    c = np.asarray(inputs["c"], f32)
    positions = np.asarray(inputs["positions"]).astype(np.int32)
    sq = lambda k: np.ascontiguousarray(np.asarray(inputs[k], f32)[0])
    shared = {
        "consts": consts,
        "w_ada": sq("w_ada"), "w_in": sq("w_in"), "w_conv_out": sq("w_conv_out"),
        "w_ret_out": sq("w_ret_out"), "w_out": sq("w_out"), "w_ff1": sq("w_ff1"), "w_ff2": sq("w_ff2"),
    }
    in_maps = []
    for b in range(NB):
        vecs = np.zeros((NVR, 128), f32)
        vecs[R_BADA:R_BADA + 96] = sq("b_ada").reshape(96, 128)
        vecs[R_C:R_C + 16] = c[b].reshape(16, 128)
        vecs[R_GMIX:R_GMIX + 16] = sq("g_norm_mix").reshape(16, 128)
        vecs[R_GFFN:R_GFFN + 16] = sq("g_norm_ffn").reshape(16, 128)
        vecs[R_GFIN:R_GFIN + 16] = np.asarray(inputs["g_norm_final"], f32).reshape(16, 128)
        vecs[R_GNG:R_GNG + 16] = sq("ret_gn_g").reshape(16, 128)
        vecs[R_GNB:R_GNB + 16] = sq("ret_gn_b").reshape(16, 128)
        vecs[R_CONVB:R_CONVB + 8] = sq("conv_b").reshape(8, 128)
        vecs[R_LNG:R_LNG + 8] = sq("conv_ln_g").reshape(8, 128)
        vecs[R_LNB:R_LNB + 8] = sq("conv_ln_b").reshape(8, 128)
        vecs[R_CONVW:R_CONVW + CONV_W * 8] = sq("conv_w").reshape(CONV_W * 8, 128)
        m = dict(shared)
        m["x"] = np.ascontiguousarray(x[b])
        m["pos"] = np.ascontiguousarray(positions[b].reshape(1, SEQ))
        m["vecs"] = vecs
        in_maps.append(m)
    return in_maps


def kernel(**inputs):
    nc = build_program()
    in_maps = make_in_maps(inputs)
    res = run_bass_kernel_spmd(nc, in_maps, core_ids=list(range(NB)))
    return np.stack([np.asarray(r["out"], np.float32) for r in res.results], axis=0)
```

```python
import numpy as np
import concourse.bass as bass
import concourse.mybir as mybir
from concourse.bass_utils import run_bass_kernel_spmd

F32 = mybir.dt.float32
BF16 = mybir.dt.bfloat16
I32 = mybir.dt.int32
AF = mybir.ActivationFunctionType
ALU = mybir.AluOpType
AX = mybir.AxisListType


class Buf:
    __slots__ = ("name", "w", "r", "psum")

    def __init__(self, name, psum=False):
        self.name = name
        self.psum = psum
        self.w = None
        self.r = []


class _Recorder:
    def __getattr__(self, name):
        def rec(*args, **kwargs):
            return (name, args, kwargs)
        return rec


E = _Recorder()


class Sched:
    ENGS = ("pe", "dve", "act", "pool", "sp")

    def __init__(self, nc, same_engine_sync=True):
        self.nc = nc
        self.h = {"pe": nc.tensor, "dve": nc.vector, "act": nc.scalar,
                  "pool": nc.gpsimd, "sp": nc.sync}
        self.ops = {e: [] for e in self.ENGS}
        self.sem = {e: nc.alloc_semaphore("sem_" + e) for e in self.ENGS}
        self.cnt = {e: 0 for e in self.ENGS}
        self.sig = {e: [] for e in self.ENGS}
        self.waited = {e: {} for e in self.ENGS}
        self.same = same_engine_sync
        self.dma_sems = {}
        self.dma_cnt = {}

    def dma_sem(self, name):
        if name not in self.dma_sems:
            self.dma_sems[name] = self.nc.alloc_semaphore("dsem_" + name)
            self.dma_cnt[name] = 0
        return name

    def _resolve(self, tok):
        if tok[0] == "d":
            return (self.dma_sems[tok[1]], "D" + tok[1], self.dma_cnt[tok[1]])
        _, e, seq = tok
        best = None
        for (s, c) in reversed(self.sig[e]):
            if s < seq:
                break
            best = (s, c)
        if best is None:
            rec = self.ops[e][-1]
            assert rec["inc"] is None and rec["fn"] is not None and len(self.ops[e]) - 1 >= seq, tok
            self.cnt[e] += 1
            rec["inc"] = (self.sem[e], 1)
            best = (len(self.ops[e]) - 1, self.cnt[e])
            self.sig[e].append(best)
        return (self.sem[e], "E" + e, best[1])

    def _deps(self, eng, reads, writes, pe_accum=False):
        toks = []
        for b in reads:
            if b.w is not None:
                toks.append(b.w)
            if b.psum:
                toks.extend(t for t in b.r if not (t[0] == "c" and t[1] == eng))
        for b in writes:
            if b.w is not None:
                toks.append(b.w)
            toks.extend(b.r)
        waits = {}
        for t in toks:
            if t[0] == "c" and t[1] == eng:
                if not self.same or eng == "pe":
                    continue
            sem, key, val = self._resolve(t)
            if self.waited[eng].get(key, 0) >= val:
                continue
            if key not in waits or waits[key][1] < val:
                waits[key] = (sem, val)
        for key, (sem, val) in waits.items():
            self.waited[eng][key] = val
        return list(waits.values())

    def _commit(self, tok, reads, writes):
        for b in reads:
            b.r = [t for t in b.r if not (t[0] == tok[0] and t[1] == tok[1])]
            b.r.append(tok)
        for b in writes:
            b.w = tok
            b.r = []

    def op(self, eng, fn, reads=(), writes=(), signal=True):
        waits = self._deps(eng, reads, writes)
        seq = len(self.ops[eng])
        rec = {"fn": fn, "waits": waits, "inc": None}
        if signal:
            self.cnt[eng] += 1
            self.sig[eng].append((seq, self.cnt[eng]))
            rec["inc"] = (self.sem[eng], 1)
        self.ops[eng].append(rec)
        tok = ("c", eng, seq)
        self._commit(tok, reads, writes)
        return tok

    def dma(self, queue, out, in_, reads=(), writes=(), sem="misc", **kw):
        self.dma_sem(sem)
        waits = self._deps(queue, reads, writes)
        self.dma_cnt[sem] += 16
        val = self.dma_cnt[sem]
        semh = self.dma_sems[sem]

        def fn(e, out=out, in_=in_, kw=kw):
            return e.dma_start(out=out, in_=in_, **kw)
        self.ops[queue].append({"fn": fn, "waits": waits, "inc": (semh, 16)})
        tok = ("d", sem, val)
        self._commit(tok, reads, writes)
        return tok

    def wait_all(self, eng, toks):
        waits = []
        for t in toks:
            sem, key, val = self._resolve(t)
            if self.waited[eng].get(key, 0) >= val:
                continue
            self.waited[eng][key] = val
            waits.append((sem, val))
        self.ops[eng].append({"fn": None, "waits": waits, "inc": None})

    def emit(self):
        nc = self.nc
        with nc.Block() as block:
            def mk(ename):
                def body(e):
                    for rec in self.ops[ename]:
                        for (sem, val) in rec["waits"]:
                            e.wait_ge(sem, val)
                        if rec["fn"] is None:
                            continue
                        fn = rec["fn"]
                        if isinstance(fn, tuple):
                            ins = getattr(e, fn[0])(*fn[1], **fn[2])
                        else:
                            ins = fn(e)
                        if rec["inc"] is not None:
                            ins.then_inc(rec["inc"][0], rec["inc"][1])
                return body
            block.tensor(mk("pe"))
            block.vector(mk("dve"))
            block.scalar(mk("act"))
            block.gpsimd(mk("pool"))
            block.sync(mk("sp"))


D = 2048
SEQ = 2048
NB = 8
CONV_DIM = 1024
CONV_W = 31
HEADS = 8
DQK = 128
DV = 256
D_FF = 8192
EPS = 1e-6
TT = 512
NCH = TT // 128
NT = SEQ // TT
KC = D // 128
HALO = CONV_W - 1
O_CONV = 2 * CONV_DIM
O_Q = O_CONV
O_K = O_Q + 1024
O_V = O_K + 1024
O_G = O_V + 2048
O_GC = O_G + 2048
O_GR = O_GC + 2048
IN_COLS = O_GR + 2048
NSLOT = 3

R_BADA, R_C, R_GMIX = 0, 96, 112
R_GFFN, R_GFIN, R_GNG, R_GNB, R_CONVB, R_LNG, R_LNB = 128, 144, 160, 176, 192, 200, 208
R_CONVW = 256
NVR = 512
C_ID, C_PERM, C_ONES, C_MASK, C_XI, C_ZETA, C_INVF = 0, 128, 256, 384, 1408, 2432, 2440
NCONST = 2441


def host_consts():
    f32 = np.float32
    c = np.zeros((128, NCONST), f32)
    c[:, C_ID:C_ID + 128] = np.eye(128, dtype=f32)
    perm = np.zeros((128, 128), f32)
    for m in range(128):
        if m < 64:
            perm[m + 64, m] = -1.0
        else:
            perm[m - 64, m] = 1.0
    c[:, C_PERM:C_PERM + 128] = perm
    c[:, C_ONES:C_ONES + 128] = 1.0
    hh = np.arange(HEADS, dtype=f32)
    log_gamma = np.log(f32(1.0) - np.exp2(f32(-5.0) - hh)).astype(f32)
    idx = np.arange(128, dtype=f32)
    scale = f32(DQK ** -0.5)
    for h in range(HEADS):
        diff = idx[None, :] - idx[:, None]
        m = np.where(diff >= 0, np.exp(np.maximum(diff, 0.0) * log_gamma[h]), 0.0).astype(f32)
        c[:, C_MASK + h * 128:C_MASK + (h + 1) * 128] = m * scale
        xi = np.exp((idx + 1.0) * log_gamma[h]).astype(f32)
        c[:, C_XI + h * 128:C_XI + (h + 1) * 128] = xi[None, :]
        zeta = np.exp((127.0 - idx) * log_gamma[h]).astype(f32)
        c[:, C_ZETA + h] = zeta * scale
    half = 64
    invf = (f32(10000.0) ** (-np.arange(0, half, dtype=f32) / f32(half))).astype(f32)
    c[:, C_INVF] = np.concatenate([invf, invf])
    cd = np.exp(f32(128.0) * log_gamma).astype(f32)
    return c, [float(v) for v in cd]


def build_program(dbg=None, stop=None):
    nc = bass.Bass("TRN2", target_bir_lowering=False)
    _, CD = host_consts()

    def din(name, shape, dt=F32):
        return nc.dram_tensor(name, shape, dt, kind="ExternalInput").ap()
    x = din("x", [SEQ, D])
    pos = din("pos", [1, SEQ], I32)
    vecs = din("vecs", [NVR, 128])
    consts = din("consts", [128, NCONST])
    w_ada = din("w_ada", [D, 6 * D])
    w_in = din("w_in", [D, IN_COLS])
    w_co = din("w_conv_out", [CONV_DIM, D])
    w_ro = din("w_ret_out", [D, D])
    w_o = din("w_out", [D, D])
    w_f1 = din("w_ff1", [D, D_FF])
    w_f2 = din("w_ff2", [D_FF, D])
    out = nc.dram_tensor("out", [SEQ, D], F32, kind="ExternalOutput").ap()
    dbg_t = {}
    if dbg:
        for k, shp in dbg.items():
            dbg_t[k] = nc.dram_tensor("dbg_" + k, shp, F32, kind="ExternalOutput").ap()

    S = Sched(nc)

    def sb(name, shape, dt=F32):
        return nc.alloc_sbuf_tensor(name, shape, dt)

    cst = sb("cst", [128, NCONST]); cstB = Buf("cst")
    vfm = sb("vfm", [128, NVR]); vfmB = Buf("vfm")
    modv = sb("modv", [128, 96]); modB = Buf("modv")
    prm = sb("prm", [128, 4, 16]); prmB = Buf("prm")
    cact = sb("cact", [128, 16], BF16); cactB = Buf("cact")
    xT = sb("xT", [128, KC, TT]); xTB = [Buf(f"xT{i}") for i in range(KC)]
    xTcB = [Buf(f"xTc{c}") for c in range(NCH)]
    hT = sb("hT", [128, KC, TT], BF16); hTB = [Buf(f"hT{i}") for i in range(KC)]
    wsl = [sb(f"wsl{i}", [128, 16, 512], BF16) for i in range(NSLOT)]
    wslB = [Buf(f"wsl{i}") for i in range(NSLOT)]
    R32 = sb("R32", [128, HEADS, DV]); R32B = [Buf(f"R32_{h}") for h in range(HEADS)]
    Rbf = sb("Rbf", [128, HEADS, DV], BF16); RbfB = [Buf(f"Rbf_{h}") for h in range(HEADS)]
    halo = sb("halo", [128, 8, HALO]); haloB = [Buf(f"halo{i}") for i in range(8)]
    cosT = sb("cosT", [128, TT]); sinT = sb("sinT", [128, TT]); trigB = Buf("trig")
    rrep = sb("rrep", [128, TT]); rrepB = Buf("rrep")
    mrep = sb("mrep", [128, TT]); mrepB = Buf("mrep")
    NTMP = 4
    tmp = [sb(f"tmp{i}", [128, TT]) for i in range(NTMP)]
    tmpB = [Buf(f"tmp{i}") for i in range(NTMP)]
    sml = sb("sml", [128, 64]); smlB = Buf("sml")
    XBYTES = 65 * 1024
    X = sb("X", [128, XBYTES // 2], BF16)
    psum = [nc.alloc_psum_tensor(f"ps{i}", [128, 512], F32) for i in range(8)]
    psB = [Buf(f"ps{i}", psum=True) for i in range(8)]
    st = {"ps": 0, "tmp": 0, "w": 0, "alt": 0}

    def newps():
        i = st["ps"]; st["ps"] = (i + 1) % 8
        return psum[i], psB[i]

    def newtmp():
        i = st["tmp"]; st["tmp"] = (i + 1) % NTMP
        return tmp[i], tmpB[i]

    def xview(off, free, dt):
        n = int(np.prod(free)); esz = 4 if dt == F32 else 2
        assert off % 4 == 0 and off + n * esz <= XBYTES, (off, free)
        a = X[:, off // 2: off // 2 + n * esz // 2]
        if dt == F32:
            a = a.bitcast(F32)
        if len(free) == 2:
            a = a.rearrange("p (a b) -> p a b", a=free[0])
        return a

    xregion = {"bufs": []}

    def xbufs(names, inherit):
        bs = []
        for n in names:
            b = Buf(n)
            b.r = list(inherit)
            bs.append(b)
        xregion["bufs"].extend(bs)
        return bs

    def xcollect():
        toks = {}
        for b in xregion["bufs"]:
            for t in ([b.w] if b.w is not None else []) + list(b.r):
                key = (t[0], t[1])
                if key not in toks or toks[key][2] < t[2]:
                    toks[key] = t
        xregion["bufs"] = []
        return list(toks.values())

    def wload(parts):
        i = st["w"]; st["w"] = (i + 1) % NSLOT
        for (ap, coff) in parts:
            K, n = ap.shape
            nk = K // 128
            S.dma("pool", wsl[i][:, 0:nk, coff:coff + n],
                  ap.rearrange("(kc p) n -> p kc n", p=128),
                  writes=[wslB[i]], sem=f"w{i}")
        return wsl[i], wslB[i]

    def copy_alt(out_ap, in_ap, reads, writes):
        st["alt"] ^= 1
        if st["alt"]:
            return S.op("dve", E.tensor_copy(out_ap, in_ap), reads=reads, writes=writes)
        return S.op("act", E.copy(out_ap, in_ap), reads=reads, writes=writes)

    ident = cst[:, C_ID:C_ID + 128]
    ones = cst[:, C_ONES:C_ONES + 128]
    permT = cst[:, C_PERM:C_PERM + 128]

    S.dma("sp", cst[:], consts, writes=[cstB], sem="c0")
    vst, vstB = xview(0, (4, 128), F32), None
    (vstB,) = xbufs(["vst"], [])
    S.dma("sp", vst, vecs.rearrange("(b p) f -> p b f", p=128), writes=[vstB], sem="c1")
    p0, p0B = newps()
    for blk in range(4):
        S.op("pe", E.transpose(p0[:, blk * 128:(blk + 1) * 128], vst[:, blk, :], ident),
             reads=[vstB, cstB], writes=[p0B], signal=(blk == 3))
    S.op("dve", E.tensor_copy(vfm[:], p0[:]), reads=[p0B], writes=[vfmB])
    S.op("act", E.activation(cact[:], vfm[:, R_C:R_C + 16], AF.Silu), reads=[vfmB], writes=[cactB])
    for hh in range(HEADS):
        S.op("dve", E.memset(R32[:, hh, :], 0.0), writes=[R32B[hh]])
        S.op("dve", E.memset(Rbf[:, hh, :], 0.0), writes=[RbfB[hh]])
    for cc in range(8):
        S.op("dve", E.memset(halo[:, cc, :], 0.0), writes=[haloB[cc]])
    def ada_part(j0, j1):
        p, pB = newps()
        for j in range(j0, j1):
            wt, wtB = wload([(w_ada[:, j * 512:(j + 1) * 512], 0)])
            for oc in range(4):
                col = (j - j0) * 4 + oc
                for kc in range(KC):
                    S.op("pe", E.matmul(
                        p[:, col:col + 1], wt[:, kc, oc * 128:(oc + 1) * 128], cact[:, kc:kc + 1],
                        start=(kc == 0), stop=(kc == KC - 1)),
                        reads=[wtB, cactB], writes=[pB], signal=(kc == KC - 1))
        n = (j1 - j0) * 4
        S.op("dve", E.tensor_tensor(modv[:, j0 * 4:j1 * 4], p[:, 0:n], vfm[:, R_BADA + j0 * 4:R_BADA + j1 * 4], ALU.add),
             reads=[pB, vfmB], writes=[modB])

    A1 = prm[:, 0, :]; B1 = modv[:, 0:16]; GM = modv[:, 32:48]
    A2 = prm[:, 1, :]; B2 = modv[:, 48:64]; GF = modv[:, 80:96]
    GFIN = vfm[:, R_GFIN:R_GFIN + 16]

    def rstd_from_ps(ps_ap, psb, inv_n, dst, dstB):
        S.op("dve", E.tensor_scalar(dst, ps_ap, inv_n, EPS, ALU.mult, ALU.add), reads=[psb], writes=[dstB])
        S.op("act", E.activation(dst, dst, AF.Sqrt), reads=[dstB], writes=[dstB])
        S.op("dve", E.reciprocal(dst, dst), reads=[dstB], writes=[dstB])

    def sumsq_rstd(src, srcB):
        pss, pssB = newps()
        for fc in range(KC):
            t, tB = newtmp()
            S.op("act", E.activation(t[:], src[:, fc, :], AF.Square), reads=[srcB[fc]], writes=[tB])
            S.op("pe", E.matmul(pss[:], ones, t[:], start=(fc == 0), stop=(fc == KC - 1)),
                 reads=[tB, cstB], writes=[pssB], signal=(fc == KC - 1))
        rstd_from_ps(pss[:], pssB, 1.0 / D, rrep[:], rrepB)

    def norm_mod(A, B, rd):
        sumsq_rstd(xT, xTB)
        for fc in range(KC):
            t, tB = newtmp()
            S.op("dve", E.scalar_tensor_tensor(t[:], xT[:, fc, :], A[:, fc:fc + 1], rrep[:],
                                                                       ALU.mult, ALU.mult),
                 reads=[xTB[fc], rrepB] + rd, writes=[tB])
            S.op("act", E.activation(hT[:, fc, :], t[:], AF.Identity, bias=B[:, fc:fc + 1]),
                 reads=[tB] + rd, writes=[hTB[fc]])

    def dump(name, ap, bufs):
        if name in dbg_t:
            S.dma("sp", dbg_t[name], ap, reads=bufs, sem="dbg")

    posb = sb("posb", [128, TT], I32); posB = Buf("posb")
    acc2 = sb("acc2", [128, TT]); acc2B = Buf("acc2")

    def make_tables(tq0):
        posi = posb[:]
        S.dma("sp", posi, pos[0:1, tq0:tq0 + TT].partition_broadcast(128), writes=[posB], sem="pos")
        ang, angB = newtmp()
        S.op("dve", E.tensor_copy(ang[:], posi), reads=[posB], writes=[angB])
        S.op("dve", E.tensor_scalar(ang[:], ang[:], cst[:, C_INVF:C_INVF + 1], None, ALU.mult),
             reads=[angB, cstB], writes=[angB])
        TWO_PI = 2.0 * np.pi
        for (dst, shift) in ((sinT, 0.0), (cosT, np.pi / 2)):
            a, aB = newtmp(); ki, kiB = newtmp(); kf, kfB = newtmp()
            kint = ki[:].bitcast(I32)
            S.op("dve", E.tensor_scalar(a[:], ang[:], float(shift), None, ALU.add),
                 reads=[angB], writes=[aB])
            S.op("dve", E.tensor_scalar(kf[:], a[:], float(1.0 / TWO_PI), None, ALU.mult),
                 reads=[aB], writes=[kfB])
            S.op("dve", E.tensor_copy(kint, kf[:]), reads=[kfB], writes=[kiB])
            S.op("dve", E.tensor_copy(kf[:], kint), reads=[kiB], writes=[kfB])
            S.op("dve", E.scalar_tensor_tensor(a[:], kf[:], float(-TWO_PI), a[:], ALU.mult, ALU.add),
                 reads=[kfB, aB], writes=[aB])
            S.op("dve", E.tensor_scalar(kf[:], a[:], float(np.pi), float(-TWO_PI), ALU.is_gt, ALU.mult),
                 reads=[aB], writes=[kfB])
            S.op("dve", E.tensor_tensor(a[:], a[:], kf[:], ALU.add), reads=[aB, kfB], writes=[aB])
            S.op("dve", E.tensor_scalar(kf[:], a[:], float(-np.pi), float(TWO_PI), ALU.is_lt, ALU.mult),
                 reads=[aB], writes=[kfB])
            S.op("dve", E.tensor_tensor(a[:], a[:], kf[:], ALU.add), reads=[aB, kfB], writes=[aB])
            S.op("dve", E.tensor_scalar(a[:], a[:], float(np.pi), float(-np.pi), ALU.min, ALU.max),
                 reads=[aB], writes=[aB])
            S.op("act", E.activation(dst[:], a[:], AF.Sin), reads=[aB], writes=[trigB])

    out_toks = []
    make_tables(0)
    pre_xin = None

    def finish():
        tk = S.dma("sp", out[0:128, :], xT[:, 0:4, :].rearrange("p a b -> p (a b)"), reads=xTB, sem="out")
        S.wait_all("sp", [tk])
        S.emit()
        return nc

    if stop == "setup":
        return finish()
    for ti in range(NT):
        t0 = ti * TT
        inh = xcollect()
        xins = [xview(40 * 1024, (D,), F32), xview(48 * 1024, (D,), F32)]
        if pre_xin is None:
            xinBs = xbufs(["xin0", "xin1"], inh)
            nloaded = 0
        else:
            xinBs = pre_xin
            for b_ in xinBs:
                b_.r.extend(inh)
            xregion["bufs"].extend(xinBs)
            nloaded = 2
        for c in range(NCH):
            xin, xinB = xins[c % 2], xinBs[c % 2]
            if c >= nloaded:
                S.dma("sp", xin, x[t0 + c * 128:t0 + (c + 1) * 128, :], writes=[xinB], sem=f"xin{c % 2}")
            for g in range(4):
                ps, pB = newps()
                for j in range(4):
                    fc = g * 4 + j
                    S.op("pe", E.transpose(ps[:, j * 128:(j + 1) * 128],
                                                                         xin[:, fc * 128:(fc + 1) * 128], ident),
                         reads=[xinB, cstB], writes=[pB], signal=(j == 3))
                copy_alt(xT[:, g * 4:(g + 1) * 4, c * 128:(c + 1) * 128],
                         ps[:].rearrange("p (a b) -> p a b", a=4), [pB], xTB[g * 4:(g + 1) * 4] + [xTcB[c]])
        if ti == 0:
            dump("xT", xT[:], xTB)
            if stop == "in":
                return finish()
        if ti == 0:
            ada_part(0, 8)
            S.op("dve", E.scalar_tensor_tensor(prm[:, 0, :], modv[:, 16:32], 1.0, vfm[:, R_GMIX:R_GMIX + 16],
                                               ALU.add, ALU.mult), reads=[modB, vfmB], writes=[prmB])
        norm_mod(A1, B1, [prmB, modB])
        if ti == 0:
            dump("hT", hT[:], hTB)
            if stop == "norm1":
                return finish()
        inh = xcollect()
        yrin = xview(49 * 1024, (KC, TT), BF16)
        yrB = xbufs([f"yr{i}" for i in range(KC)], inh)
        qk32 = xview(0, (4, TT), F32)
        qrot = xview(8 * 1024, (2, TT), BF16)
        qxrot = xview(10 * 1024, (2, TT), BF16)
        krot = xview(12 * 1024, (2, TT), BF16)
        ktm = xview(14 * 1024, (NCH, 256), BF16)
        vtm = xview(16 * 1024, (NCH, 512), BF16)
        vz = xview(20 * 1024, (NCH, 512), BF16)
        o32 = xview(24 * 1024, (NCH, 512), F32)
        smk = xview(32 * 1024, (2, 256), BF16)
        qkB = xbufs([f"qk{j}" for j in range(4)], inh)
        qrB = xbufs(["qr0", "qr1"], inh); qxB = xbufs(["qx0", "qx1"], inh); krB = xbufs(["kr0", "kr1"], inh)
        ktB = xbufs([f"kt{c}" for c in range(NCH)], inh)
        vtB = xbufs([f"vt{c}" for c in range(NCH)], inh)
        vzB = xbufs([f"vz{c}" for c in range(NCH)], inh)
        oB = xbufs([f"o{c}" for c in range(NCH)], inh)
        smB = xbufs(["sm0", "sm1"], inh)
        sgv = xview(33 * 1024, (4, TT), F32)
        sgB = xbufs([f"sg{j}" for j in range(4)], inh)
        rt = xview(41 * 1024, (4, TT), F32)
        rtB = xbufs([f"rt{j}" for j in range(4)], inh)

        def qk_proj(hp):
            wt, wtB = wload([(w_in[:, O_Q + hp * 256:O_Q + (hp + 1) * 256], 0),
                             (w_in[:, O_K + hp * 256:O_K + (hp + 1) * 256], 256)])
            for j in range(4):
                p, pB = newps()
                for kc in range(KC):
                    S.op("pe", E.matmul(
                        p[:], wt[:, kc, j * 128:(j + 1) * 128], hT[:, kc, :], start=(kc == 0), stop=(kc == KC - 1)),
                        reads=[wtB, hTB[kc]], writes=[pB], signal=(kc == KC - 1))
                copy_alt(qk32[:, j, :], p[:], [pB], [qkB[j]])

        qk_proj(0)
        for hp in range(4):
            if ti == 0:
                ada_part(8 + hp * 4, 12 + hp * 4)
                if hp == 2:
                    S.op("dve", E.scalar_tensor_tensor(prm[:, 1, :], modv[:, 64:80], 1.0, vfm[:, R_GFFN:R_GFFN + 16],
                                                       ALU.add, ALU.mult), reads=[modB, vfmB], writes=[prmB])
            pperm = []
            for j in range(4):
                p, pB = newps()
                S.op("pe", E.matmul(p[:], permT, qk32[:, j, :], start=True, stop=True),
                     reads=[qkB[j], cstB], writes=[pB])
                pperm.append((p, pB))
            wtv, wtvB = wload([(w_in[:, O_V + hp * 512:O_V + (hp + 1) * 512], 0)])
            for j in range(4):
                p, pB = pperm[j]
                i1, i2 = 2 * (j % 2), 2 * (j % 2) + 1
                t1B, t2B = rtB[i1], rtB[i2]
                S.op("dve", E.tensor_tensor(rt[:, i1, :], qk32[:, j, :], cosT[:], ALU.mult),
                     reads=[qkB[j], trigB], writes=[t1B])
                S.op("dve", E.tensor_tensor(rt[:, i2, :], p[:], sinT[:], ALU.mult),
                     reads=[pB, trigB], writes=[t2B])
                S.op("dve", E.tensor_tensor(qk32[:, j, :], rt[:, i1, :], rt[:, i2, :], ALU.add),
                     reads=[t1B, t2B], writes=[qkB[j]])
                if j < 2:
                    head = hp * 2 + j
                    S.op("act", E.copy(qrot[:, j, :], qk32[:, j, :]), reads=[qkB[j]], writes=[qrB[j]])
                    xi_b = cst[:, C_XI + head * 128:C_XI + (head + 1) * 128].unsqueeze(1).to_broadcast([128, NCH, 128])
                    S.op("dve", E.tensor_tensor(
                        qxrot[:, j, :].rearrange("p (c q) -> p c q", c=NCH),
                        qk32[:, j, :].rearrange("p (c q) -> p c q", c=NCH), xi_b, ALU.mult),
                        reads=[qkB[j], cstB], writes=[qxB[j]])
                else:
                    S.op("act", E.copy(krot[:, j - 2, :], qk32[:, j, :]), reads=[qkB[j]], writes=[krB[j - 2]])
            for c in range(NCH):
                p, pB = newps()
                for kc in range(KC):
                    S.op("pe", E.matmul(
                        p[:], hT[:, kc, c * 128:(c + 1) * 128], wtv[:, kc, :], start=(kc == 0), stop=(kc == KC - 1)),
                        reads=[wtvB, hTB[kc]], writes=[pB], signal=(kc == KC - 1))
                if c % 2 == 0:
                    S.op("act", E.copy(vtm[:, c, :], p[:]), reads=[pB], writes=[vtB[c]])
                else:
                    S.op("dve", E.tensor_copy(vtm[:, c, :], p[:]), reads=[pB], writes=[vtB[c]])
                for hh in range(2):
                    head = hp * 2 + hh
                    zc = cst[:, C_ZETA + head:C_ZETA + head + 1]
                    if c % 2 == 0:
                        S.op("act", E.mul(vz[:, c, hh * 256:(hh + 1) * 256], p[:, hh * 256:(hh + 1) * 256], zc),
                             reads=[pB, cstB], writes=[vzB[c]])
                    else:
                        S.op("dve", E.tensor_scalar(
                            vz[:, c, hh * 256:(hh + 1) * 256], p[:, hh * 256:(hh + 1) * 256], zc, None, ALU.mult),
                            reads=[pB, cstB], writes=[vzB[c]])
            for c in range(NCH):
                p, pB = newps()
                for hh in range(2):
                    S.op("pe", E.transpose(p[:, hh * 128:(hh + 1) * 128],
                                           qk32[:, 2 + hh, c * 128:(c + 1) * 128], ident),
                         reads=[qkB[2 + hh], cstB], writes=[pB], signal=(hh == 1))
                copy_alt(ktm[:, c, :], p[:, 0:256], [pB], [ktB[c]])
            wtg, wtgB = wload([(w_in[:, O_G + hp * 512:O_G + (hp + 1) * 512], 0)])
            S.op("dve", E.memset(sml[:, 0:48], 0.0), writes=[smlB], reads=[])
            for c in range(NCH):
                psc, pscB = newps()
                for hh in range(2):
                    S.op("pe", E.matmul(
                        psc[:, hh * 128:(hh + 1) * 128], krot[:, hh, c * 128:(c + 1) * 128],
                        qrot[:, hh, c * 128:(c + 1) * 128], start=True, stop=True),
                        reads=[krB[hh], qrB[hh]], writes=[pscB], signal=(hh == 1))
                si = c % 2
                S.op("dve", E.tensor_tensor(
                    smk[:, si, :], psc[:, 0:256], cst[:, C_MASK + hp * 256:C_MASK + (hp + 1) * 256], ALU.mult),
                    reads=[pscB, cstB], writes=[smB[si]])
                pk, pkB = newps()
                for hh in range(2):
                    S.op("pe", E.matmul(
                        pk[:, hh * 256:(hh + 1) * 256], ktm[:, c, hh * 128:(hh + 1) * 128],
                        vz[:, c, hh * 256:(hh + 1) * 256], start=True, stop=True),
                        reads=[ktB[c], vzB[c]], writes=[pkB], signal=(hh == 1))
                po, poB = newps()
                for hh in range(2):
                    head = hp * 2 + hh
                    S.op("pe", E.matmul(
                        po[:, hh * 256:(hh + 1) * 256], smk[:, si, hh * 128:(hh + 1) * 128],
                        vtm[:, c, hh * 256:(hh + 1) * 256], start=True, stop=False),
                        reads=[smB[si], vtB[c]], writes=[poB], signal=False)
                    S.op("pe", E.matmul(
                        po[:, hh * 256:(hh + 1) * 256], qxrot[:, hh, c * 128:(c + 1) * 128],
                        Rbf[:, head, :], start=False, stop=True),
                        reads=[qxB[hh], RbfB[head]], writes=[poB], signal=True)
                for hh in range(2):
                    head = hp * 2 + hh
                    S.op("dve", E.scalar_tensor_tensor(
                        R32[:, head, :], R32[:, head, :], CD[head], pk[:, hh * 256:(hh + 1) * 256], ALU.mult, ALU.add),
                        reads=[pkB], writes=[R32B[head]])
                    S.op("act", E.copy(Rbf[:, head, :], R32[:, head, :]),
                         reads=[R32B[head]], writes=[RbfB[head]])
                S.op("act", E.copy(o32[:, c, :], po[:]), reads=[poB], writes=[oB[c]])
                vc = c
                p, pB = newps()
                for kc in range(KC):
                    S.op("pe", E.matmul(
                        p[:], wtg[:, kc, vc * 128:(vc + 1) * 128], hT[:, kc, :], start=(kc == 0), stop=(kc == KC - 1)),
                        reads=[wtgB, hTB[kc]], writes=[pB], signal=(kc == KC - 1))
                S.op("act", E.activation(sgv[:, vc, :], p[:], AF.Silu), reads=[pB], writes=[sgB[vc]])
                for hh in range(2):
                    mi = 16 + (c * 2 + hh) * 2
                    S.op("dve", E.bn_stats(sml[:, hh * 6:(hh + 1) * 6], o32[:, c, hh * 256:(hh + 1) * 256]),
                         reads=[oB[c]], writes=[smlB])
                    S.op("dve", E.bn_aggr(sml[:, mi:mi + 2], sml[:, hh * 6:(hh + 1) * 6]),
                         reads=[smlB], writes=[smlB])
            if hp < 3:
                qk_proj(hp + 1)
            S.op("dve", E.tensor_scalar(sml[:, 32:40], sml[:, 17:33:2], EPS, None, ALU.add), reads=[smlB], writes=[smlB])
            S.op("act", E.activation(sml[:, 32:40], sml[:, 32:40], AF.Sqrt), reads=[smlB], writes=[smlB])
            S.op("dve", E.reciprocal(sml[:, 32:40], sml[:, 32:40]), reads=[smlB], writes=[smlB])
            for c in range(NCH):
                for hh in range(2):
                    mi = 16 + (c * 2 + hh) * 2
                    ri = 32 + c * 2 + hh
                    S.op("dve", E.tensor_scalar(
                        o32[:, c, hh * 256:(hh + 1) * 256], o32[:, c, hh * 256:(hh + 1) * 256],
                        sml[:, mi:mi + 1], sml[:, ri:ri + 1], ALU.subtract, ALU.mult),
                        reads=[smlB], writes=[oB[c]])
            for vc in range(4):
                fcv = hp * 4 + vc
                pt, ptB = newps()
                for c in range(NCH):
                    S.op("pe", E.transpose(pt[:, c * 128:(c + 1) * 128],
                                           o32[:, c, vc * 128:(vc + 1) * 128], ident),
                         reads=[oB[c], cstB], writes=[ptB], signal=(c == NCH - 1))
                t, tB = newtmp()
                S.op("dve", E.tensor_scalar(
                    t[:], pt[:], vfm[:, R_GNG + fcv:R_GNG + fcv + 1], vfm[:, R_GNB + fcv:R_GNB + fcv + 1], ALU.mult, ALU.add),
                    reads=[ptB, vfmB], writes=[tB])
                S.op("dve", E.tensor_tensor(yrin[:, fcv, :], t[:], sgv[:, vc, :], ALU.mult),
                     reads=[tB, sgB[vc]], writes=[yrB[fcv]])
        if ti == 0:
            if stop == "m2":
                return finish()
        keep = yrB
        xregion["bufs"] = [b for b in xregion["bufs"] if b not in keep]
        inh = xcollect()
        xregion["bufs"].extend(keep)
        merged = xview(0, (KC, TT), BF16)
        VB_OFF = 16 * 1024
        vbuf = xview(VB_OFF, (8, HALO + TT), F32)
        acc = xview(VB_OFF + 8 * (HALO + TT) * 4, (8, TT), F32)
        mgB = xbufs([f"mg{i}" for i in range(KC)], inh)
        vbB = xbufs([f"vb{i}" for i in range(8)], inh)
        accB = xbufs([f"acc{i}" for i in range(8)], inh)

        def conv_taps(cc):
            wcol = lambda k, cc=cc: vfm[:, R_CONVW + k * 8 + cc:R_CONVW + k * 8 + cc + 1]
            a2, a2B = acc2, acc2B
            S.op("dve", E.tensor_scalar(acc[:, cc, :], vbuf[:, cc, 0:TT], wcol(0),
                                        vfm[:, R_CONVB + cc:R_CONVB + cc + 1], ALU.mult, ALU.add),
                 reads=[vbB[cc], vfmB], writes=[accB[cc]])
            S.op("dve", E.tensor_scalar(a2[:], vbuf[:, cc, 1:1 + TT], wcol(1), None, ALU.mult),
                 reads=[vbB[cc], vfmB], writes=[a2B])
            for k in range(2, CONV_W):
                if k % 2 == 0:
                    S.op("dve", E.scalar_tensor_tensor(
                        acc[:, cc, :], vbuf[:, cc, k:k + TT], wcol(k), acc[:, cc, :], ALU.mult, ALU.add),
                        reads=[vbB[cc], vfmB], writes=[accB[cc]])
                else:
                    S.op("dve", E.scalar_tensor_tensor(
                        a2[:], vbuf[:, cc, k:k + TT], wcol(k), a2[:], ALU.mult, ALU.add),
                        reads=[vbB[cc], vfmB], writes=[a2B])
            S.op("dve", E.tensor_tensor(acc[:, cc, :], acc[:, cc, :], a2[:], ALU.add),
                 reads=[a2B], writes=[accB[cc]])

        for cp in range(4):
            wt, wtB = wload([(w_in[:, cp * 256:(cp + 1) * 256], 0),
                             (w_in[:, CONV_DIM + cp * 256:CONV_DIM + (cp + 1) * 256], 256)])
            pp = [newps() for _ in range(4)]
            for j in range(4):
                for kc in range(KC):
                    S.op("pe", E.matmul(
                        pp[j][0][:], wt[:, kc, j * 128:(j + 1) * 128], hT[:, kc, :], start=(kc == 0), stop=(kc == KC - 1)),
                        reads=[wtB, hTB[kc]], writes=[pp[j][1]], signal=(kc == KC - 1))
            for j in range(2):
                cc = cp * 2 + j
                sg, sgB = newtmp()
                S.op("act", E.activation(sg[:], pp[2 + j][0][:], AF.Sigmoid), reads=[pp[2 + j][1]], writes=[sgB])
                S.op("dve", E.tensor_copy(vbuf[:, cc, 0:HALO], halo[:, cc, :]), reads=[haloB[cc]], writes=[vbB[cc]])
                S.op("dve", E.tensor_tensor(vbuf[:, cc, HALO:HALO + TT], pp[j][0][:], sg[:], ALU.mult),
                     reads=[pp[j][1], sgB], writes=[vbB[cc]])
                S.op("dve", E.tensor_copy(halo[:, cc, :], vbuf[:, cc, TT:TT + HALO]), reads=[vbB[cc]], writes=[haloB[cc]])
            cg = cp
            wtg, wtgB = wload([(w_in[:, O_GR + cg * 512:O_GR + (cg + 1) * 512], 0)])
            wtr, wtrB = wload([(w_ro[:, cg * 512:(cg + 1) * 512], 0)])
            for oc in range(4):
                fc = cg * 4 + oc
                pg, pgB = newps()
                for kc in range(KC):
                    S.op("pe", E.matmul(pg[:], wtg[:, kc, oc * 128:(oc + 1) * 128], hT[:, kc, :],
                                        start=(kc == 0), stop=(kc == KC - 1)),
                         reads=[wtgB, hTB[kc]], writes=[pgB], signal=(kc == KC - 1))
                gt, gtB = newtmp()
                S.op("act", E.activation(gt[:], pg[:], AF.Sigmoid), reads=[pgB], writes=[gtB])
                py, pyB = newps()
                for kc in range(KC):
                    S.op("pe", E.matmul(py[:], wtr[:, kc, oc * 128:(oc + 1) * 128], yrin[:, kc, :],
                                        start=(kc == 0), stop=(kc == KC - 1)),
                         reads=[wtrB, yrB[kc]], writes=[pyB], signal=(kc == KC - 1))
                S.op("dve", E.tensor_tensor(merged[:, fc, :], py[:], gt[:], ALU.mult),
                     reads=[pyB, gtB], writes=[mgB[fc]])
                if oc == 1:
                    conv_taps(cp * 2)
            conv_taps(cp * 2 + 1)
        ycin = xview(VB_OFF, (8, TT), BF16)
        vtoks = {}
        for b_ in vbB:
            for t_ in ([b_.w] if b_.w is not None else []) + list(b_.r):
                key = (t_[0], t_[1])
                if key not in vtoks or vtoks[key][2] < t_[2]:
                    vtoks[key] = t_
        ycB = xbufs([f"yc{i}" for i in range(8)], list(vtoks.values()))
        gc0 = xview(VB_OFF + 8 * 1024, (4, TT), F32)
        gc0B = xbufs([f"gc0{i}" for i in range(4)], list(vtoks.values()))
        wtg0, wtg0B = wload([(w_in[:, O_GC:O_GC + 512], 0)])
        for oc in range(4):
            pg, pgB = newps()
            for kc in range(KC):
                S.op("pe", E.matmul(pg[:], wtg0[:, kc, oc * 128:(oc + 1) * 128], hT[:, kc, :],
                                    start=(kc == 0), stop=(kc == KC - 1)),
                     reads=[wtg0B, hTB[kc]], writes=[pgB], signal=(kc == KC - 1))
            S.op("act", E.activation(gc0[:, oc, :], pg[:], AF.Sigmoid), reads=[pgB], writes=[gc0B[oc]])
        ps1, ps1B = newps(); ps2, ps2B = newps()
        for cc in range(8):
            S.op("pe", E.matmul(ps1[:], ones, acc[:, cc, :], start=(cc == 0), stop=(cc == 7)),
                 reads=[accB[cc], cstB], writes=[ps1B], signal=(cc == 7))
        for cc in range(8):
            t, tB = newtmp()
            S.op("act", E.activation(t[:], acc[:, cc, :], AF.Square), reads=[accB[cc]], writes=[tB])
            S.op("pe", E.matmul(ps2[:], ones, t[:], start=(cc == 0), stop=(cc == 7)),
                 reads=[tB, cstB], writes=[ps2B], signal=(cc == 7))
        S.op("dve", E.tensor_scalar(mrep[:], ps1[:], 1.0 / CONV_DIM, None, ALU.mult), reads=[ps1B], writes=[mrepB])
        msq, msqB = newtmp()
        S.op("dve", E.tensor_tensor(msq[:], mrep[:], mrep[:], ALU.mult), reads=[mrepB], writes=[msqB])
        S.op("dve", E.scalar_tensor_tensor(rrep[:], ps2[:], 1.0 / CONV_DIM, msq[:], ALU.mult, ALU.subtract),
             reads=[ps2B, msqB], writes=[rrepB])
        S.op("dve", E.tensor_scalar(rrep[:], rrep[:], 0.0, EPS, ALU.max, ALU.add), reads=[rrepB], writes=[rrepB])
        S.op("act", E.activation(rrep[:], rrep[:], AF.Sqrt), reads=[rrepB], writes=[rrepB])
        S.op("dve", E.reciprocal(rrep[:], rrep[:]), reads=[rrepB], writes=[rrepB])
        for cc in range(8):
            t, tB = newtmp()
            S.op("dve", E.tensor_tensor(t[:], acc[:, cc, :], mrep[:], ALU.subtract),
                 reads=[accB[cc], mrepB], writes=[tB])
            S.op("dve", E.tensor_tensor(t[:], t[:], rrep[:], ALU.mult), reads=[tB, rrepB], writes=[tB])
            S.op("act", E.activation(ycin[:, cc, :], t[:], AF.Silu,
                                     bias=vfm[:, R_LNB + cc:R_LNB + cc + 1],
                                     scale=vfm[:, R_LNG + cc:R_LNG + cc + 1]),
                 reads=[tB, vfmB], writes=[ycB[cc]])
        for cg in range(4):
            if cg > 0:
                wtg, wtgB = wload([(w_in[:, O_GC + cg * 512:O_GC + (cg + 1) * 512], 0)])
            wtc, wtcB = wload([(w_co[:, cg * 512:(cg + 1) * 512], 0)])
            for oc in range(4):
                fc = cg * 4 + oc
                if cg > 0:
                    pg, pgB = newps()
                    for kc in range(KC):
                        S.op("pe", E.matmul(pg[:], wtg[:, kc, oc * 128:(oc + 1) * 128], hT[:, kc, :],
                                            start=(kc == 0), stop=(kc == KC - 1)),
                             reads=[wtgB, hTB[kc]], writes=[pgB], signal=(kc == KC - 1))
                    gt, gtB = newtmp()
                    S.op("act", E.activation(gt[:], pg[:], AF.Sigmoid), reads=[pgB], writes=[gtB])
                    gta = gt[:]
                else:
                    gta, gtB = gc0[:, oc, :], gc0B[oc]
                py, pyB = newps()
                for kc in range(8):
                    S.op("pe", E.matmul(py[:], wtc[:, kc, oc * 128:(oc + 1) * 128], ycin[:, kc, :],
                                        start=(kc == 0), stop=(kc == 7)),
                         reads=[wtcB, ycB[kc]], writes=[pyB], signal=(kc == 7))
                t, tB = newtmp()
                S.op("dve", E.tensor_tensor(t[:], py[:], gta, ALU.mult), reads=[pyB, gtB], writes=[tB])
                S.op("dve", E.tensor_tensor(merged[:, fc, :], t[:], merged[:, fc, :], ALU.add),
                     reads=[tB], writes=[mgB[fc]])
        for cg in range(4):
            wt, wtB = wload([(w_o[:, cg * 512:(cg + 1) * 512], 0)])
            for oc in range(4):
                fc = cg * 4 + oc
                p, pB = newps()
                for kc in range(KC):
                    S.op("pe", E.matmul(
                        p[:], wt[:, kc, oc * 128:(oc + 1) * 128], merged[:, kc, :], start=(kc == 0), stop=(kc == KC - 1)),
                        reads=[wtB, mgB[kc]], writes=[pB], signal=(kc == KC - 1))
                S.op("dve", E.scalar_tensor_tensor(
                    xT[:, fc, :], p[:], GM[:, fc:fc + 1], xT[:, fc, :], ALU.mult, ALU.add),
                    reads=[pB, modB], writes=[xTB[fc]])
        if ti == 0:
            dump("x1", xT[:], xTB)
            if stop == "m3":
                return finish()
        norm_mod(A2, B2, [prmB, modB])
        if ti + 1 < NT:
            make_tables((ti + 1) * TT)
        inh = xcollect()
        hid = xview(0, (64, TT), BF16)
        hidB = xbufs([f"hid{i}" for i in range(64)], inh)
        for j in range(16):
            wt, wtB = wload([(w_f1[:, j * 512:(j + 1) * 512], 0)])
            for oc in range(4):
                hc = j * 4 + oc
                p, pB = newps()
                for kc in range(KC):
                    S.op("pe", E.matmul(
                        p[:], wt[:, kc, oc * 128:(oc + 1) * 128], hT[:, kc, :], start=(kc == 0), stop=(kc == KC - 1)),
                        reads=[wtB, hTB[kc]], writes=[pB], signal=(kc == KC - 1))
                t, tB = newtmp()
                S.op("act", E.activation(t[:], p[:], AF.Relu), reads=[pB], writes=[tB])
                S.op("dve", E.tensor_tensor(hid[:, hc, :], t[:], t[:], ALU.mult),
                     reads=[tB], writes=[hidB[hc]])
        for cg in range(4):
            pp = [newps() for _ in range(4)]
            for kq in range(4):
                wt, wtB = wload([(w_f2[kq * 2048:(kq + 1) * 2048, cg * 512:(cg + 1) * 512], 0)])
                for oc in range(4):
                    for kc in range(KC):
                        first = (kq == 0 and kc == 0); last = (kq == 3 and kc == KC - 1)
                        S.op("pe", E.matmul(
                            pp[oc][0][:], wt[:, kc, oc * 128:(oc + 1) * 128], hid[:, kq * 16 + kc, :], start=first, stop=last),
                            reads=[wtB, hidB[kq * 16 + kc]], writes=[pp[oc][1]], signal=(kc == KC - 1))
            for oc in range(4):
                fc = cg * 4 + oc
                S.op("dve", E.scalar_tensor_tensor(
                    xT[:, fc, :], pp[oc][0][:], GF[:, fc:fc + 1], xT[:, fc, :], ALU.mult, ALU.add),
                    reads=[pp[oc][1], modB], writes=[xTB[fc]])
        if ti == 0 and stop == "ffn":
            return finish()
        inh_end = xcollect()
        pre_xin = None
        if ti + 1 < NT:
            pre_xin = xbufs(["xin0", "xin1"], inh_end)
            for c in range(2):
                S.dma("sp", xview((40 + 8 * c) * 1024, (D,), F32),
                      x[t0 + TT + c * 128:t0 + TT + (c + 1) * 128, :], writes=[pre_xin[c]], sem=f"xin{c}")
        sumsq_rstd(xT, xTB)
        for fc in range(KC):
            S.op("dve", E.scalar_tensor_tensor(xT[:, fc, :], xT[:, fc, :], GFIN[:, fc:fc + 1], rrep[:],
                                                                ALU.mult, ALU.mult),
                 reads=[rrepB, vfmB], writes=[xTB[fc]] + xTcB)
        ost = xview(0, (4, D), F32)
        ostB = xbufs([f"ost{i}" for i in range(4)], inh_end)
        for c in range(NCH):
            oi = c
            for g in range(4):
                ps, pB = newps()
                for j in range(4):
                    fc = g * 4 + j
                    S.op("pe", E.transpose(ps[:, j * 128:(j + 1) * 128],
                                                                              xT[:, fc, c * 128:(c + 1) * 128], ident),
                         reads=[xTcB[c], cstB], writes=[pB], signal=(j == 3))
                copy_alt(ost[:, oi, g * 512:(g + 1) * 512], ps[:], [pB], [ostB[oi]])
            out_toks.append(S.dma("sp", out[t0 + c * 128:t0 + (c + 1) * 128, :], ost[:, oi, :], reads=[ostB[oi]], sem="out"))
    S.wait_all("sp", out_toks[-1:])
    S.emit()
    return nc


def make_in_maps(inputs):
    f32 = np.float32
    consts, _ = host_consts()
    x = np.asarray(inputs["x"], f32)
    c = np.asarray(inputs["c"], f32)
    positions = np.asarray(inputs["positions"]).astype(np.int32)
    sq = lambda k: np.ascontiguousarray(np.asarray(inputs[k], f32)[0])
    shared = {
        "consts": consts,
        "w_ada": sq("w_ada"), "w_in": sq("w_in"), "w_conv_out": sq("w_conv_out"),
        "w_ret_out": sq("w_ret_out"), "w_out": sq("w_out"), "w_ff1": sq("w_ff1"), "w_ff2": sq("w_ff2"),
    }
    in_maps = []
    for b in range(NB):
        vecs = np.zeros((NVR, 128), f32)
        vecs[R_BADA:R_BADA + 96] = sq("b_ada").reshape(96, 128)
        vecs[R_C:R_C + 16] = c[b].reshape(16, 128)
        vecs[R_GMIX:R_GMIX + 16] = sq("g_norm_mix").reshape(16, 128)
        vecs[R_GFFN:R_GFFN + 16] = sq("g_norm_ffn").reshape(16, 128)
        vecs[R_GFIN:R_GFIN + 16] = np.asarray(inputs["g_norm_final"], f32).reshape(16, 128)
        vecs[R_GNG:R_GNG + 16] = sq("ret_gn_g").reshape(16, 128)
        vecs[R_GNB:R_GNB + 16] = sq("ret_gn_b").reshape(16, 128)
        vecs[R_CONVB:R_CONVB + 8] = sq("conv_b").reshape(8, 128)
        vecs[R_LNG:R_LNG + 8] = sq("conv_ln_g").reshape(8, 128)
        vecs[R_LNB:R_LNB + 8] = sq("conv_ln_b").reshape(8, 128)
        vecs[R_CONVW:R_CONVW + CONV_W * 8] = sq("conv_w").reshape(CONV_W * 8, 128)
        m = dict(shared)
        m["x"] = np.ascontiguousarray(x[b])
        m["pos"] = np.ascontiguousarray(positions[b].reshape(1, SEQ))
        m["vecs"] = vecs
        in_maps.append(m)
    return in_maps


def kernel(**inputs):
    nc = build_program()
    in_maps = make_in_maps(inputs)
    res = run_bass_kernel_spmd(nc, in_maps, core_ids=list(range(NB)))
    return np.stack([np.asarray(r["out"], np.float32) for r in res.results], axis=0)
```
